# Optimizing a Trainium2 kernel written in Bass

```python
import jax, jax.numpy as jnp
from jax import lax
import numpy as np

D_MODEL = 1024
BATCH = 8
SEQ = 4096
DEPTH = 1

SB_HEADS = 8
SB_HEAD_DIM = 64
SB_WIDTH = SB_HEADS * SB_HEAD_DIM
Q_BLOCK = 128
HG_HEADS = 4
HG_KEY_DIM = 128
HG_VAL_DIM = 128
HG_KEY_WIDTH = HG_HEADS * HG_KEY_DIM
HG_WIDTH = HG_HEADS * HG_VAL_DIM
HG_CHUNK = 64
MIX_WIDTH = SB_WIDTH + HG_WIDTH
FFN_HIDDEN = -(-8 * D_MODEL // (3 * 256)) * 256
IN_SIZES = (SB_WIDTH, SB_WIDTH, SB_WIDTH, HG_KEY_WIDTH, HG_KEY_WIDTH,
            HG_WIDTH, HG_WIDTH, D_MODEL, D_MODEL)
IN_WIDTH = sum(IN_SIZES)
EPS = 1e-6

kernel_name = "stickbreaking_hgrn2_gated_hybrid_block"


def rms_norm(x, g):
    xf = x.astype(jnp.float32)
    y = xf * lax.rsqrt(jnp.mean(xf * xf, axis=-1, keepdims=True) + EPS)
    return (y * g.astype(jnp.float32)).astype(x.dtype)


def split_cols(t, sizes):
    idx, acc = [], 0
    for s in sizes[:-1]:
        acc += s
        idx.append(acc)
    return jnp.split(t, idx, axis=-1)


def stick_breaking_attention(q, k, v):
    seq = q.shape[2]
    scale = SB_HEAD_DIM ** -0.5
    outs = []
    for blk in range(seq // Q_BLOCK):
        t0 = blk * Q_BLOCK
        end = t0 + Q_BLOCK
        qb, kb, vb = q[:, :, t0:end], k[:, :, :end], v[:, :, :end]
        z = jnp.einsum('bhtd,bhsd->bhts', qb, kb).astype(jnp.float32) * scale
        t_idx = t0 + jnp.arange(Q_BLOCK)[:, None]
        s_idx = jnp.arange(end)[None, :]
        causal = s_idx < t_idx
        sp = jnp.where(causal, jax.nn.softplus(z), 0.0)
        rest = lax.cumsum(sp, axis=3, reverse=True) - sp
        w = jnp.where(causal, jnp.exp(jax.nn.log_sigmoid(z) - rest), 0.0)
        outs.append(jnp.einsum('bhts,bhsd->bhtd', w.astype(vb.dtype), vb))
    return jnp.concatenate(outs, axis=2)


def hgrn2_mixer(q_in, f_in, i_in, g_in, lb, norm_g):
    B, S, _ = q_in.shape
    n_chunks = S // HG_CHUNK
    f = lb + (1.0 - lb) * jax.nn.sigmoid(f_in.astype(jnp.float32))
    log_f = jnp.log(f)
    key = 1.0 - f
    q = jax.nn.silu(q_in.astype(jnp.float32))
    val = i_in.astype(jnp.float32)

    def to_chunks(t):
        return t.reshape(B, n_chunks, HG_CHUNK, HG_HEADS, -1).transpose(1, 0, 3, 2, 4)

    mask = jnp.tril(jnp.ones((HG_CHUNK, HG_CHUNK), dtype=bool))[:, :, None]

    def step(state, xs):
        qc, kc, vc, gc = xs
        b = jnp.cumsum(gc, axis=2)
        diff = b[:, :, :, None, :] - b[:, :, None, :, :]
        decay = jnp.exp(jnp.where(mask, diff, -jnp.inf))
        attn = jnp.einsum('bhtk,bhsk,bhtsk->bhts', qc, kc, decay)
        o_intra = jnp.einsum('bhts,bhsv->bhtv', attn, vc)
        o_inter = jnp.einsum('bhtk,bhkv->bhtv', qc * jnp.exp(b), state)
        b_last = b[:, :, -1:, :]
        new_state = jnp.exp(b_last[:, :, 0, :])[..., None] * state + \
            jnp.einsum('bhsk,bhsv->bhkv', kc * jnp.exp(b_last - b), vc)
        return new_state, o_intra + o_inter

    s0 = jnp.zeros((B, HG_HEADS, HG_KEY_DIM, HG_VAL_DIM), jnp.float32)
    _, o = lax.scan(step, s0, (to_chunks(q), to_chunks(key), to_chunks(val), to_chunks(log_f)))
    o = o.transpose(1, 0, 3, 2, 4).reshape(B, S, HG_HEADS, HG_VAL_DIM)
    o = rms_norm(o, norm_g)
    gate = jax.nn.silu(g_in.astype(jnp.float32)).reshape(B, S, HG_HEADS, HG_VAL_DIM)
    return (o * gate).reshape(B, S, HG_WIDTH).astype(q_in.dtype)


def hybrid_layer(x, norm1_g, w_in, q_norm_g, k_norm_g, lb, hg_norm_g, w_branch, w_out,
                 norm2_g, w_gate, w_up, w_down):
    B, S, _ = x.shape
    xn = rms_norm(x, norm1_g)
    proj = xn @ w_in
    q_sb, k_sb, v_sb, q_hg, f_hg, i_hg, g_hg, gate_sb, gate_hg = split_cols(proj, IN_SIZES)

    def heads(t):
        return t.reshape(B, S, SB_HEADS, SB_HEAD_DIM).transpose(0, 2, 1, 3)

    q = rms_norm(heads(q_sb), q_norm_g)
    k = rms_norm(heads(k_sb), k_norm_g)
    o_sb = stick_breaking_attention(q, k, heads(v_sb))
    o_sb = o_sb.transpose(0, 2, 1, 3).reshape(B, S, SB_WIDTH)

    o_hg = hgrn2_mixer(q_hg, f_hg, i_hg, g_hg, lb, hg_norm_g)

    y_sb = o_sb @ w_branch[:SB_WIDTH]
    y_hg = o_hg @ w_branch[SB_WIDTH:]
    mixed = jax.nn.sigmoid(gate_sb) * y_sb + jax.nn.sigmoid(gate_hg) * y_hg
    h = x + mixed @ w_out

    hn = rms_norm(h, norm2_g)
    ffn = (jax.nn.silu(hn @ w_gate) * (hn @ w_up)) @ w_down
    return h + ffn


def setup_inputs(seed: int = 0) -> dict:
    key = jax.random.key(seed)
    ks = jax.random.split(key, 13)
    n = jax.random.normal
    f32 = jnp.float32
    return {
        "x": n(ks[0], (BATCH, SEQ, D_MODEL), f32),
        "norm1_g": 1.0 + 0.02 * n(ks[1], (DEPTH, D_MODEL), f32),
        "w_in": n(ks[2], (DEPTH, D_MODEL, IN_WIDTH), f32) * D_MODEL ** -0.5,
        "q_norm_g": 1.0 + 0.02 * n(ks[3], (DEPTH, SB_HEAD_DIM), f32),
        "k_norm_g": 1.0 + 0.02 * n(ks[4], (DEPTH, SB_HEAD_DIM), f32),
        "lb_table": 0.5 * n(ks[5], (DEPTH + 1, HG_KEY_WIDTH), f32),
        "hg_norm_g": 1.0 + 0.02 * n(ks[6], (DEPTH, HG_VAL_DIM), f32),
        "w_branch": n(ks[7], (DEPTH, MIX_WIDTH, D_MODEL), f32) * SB_WIDTH ** -0.5,
        "w_out": n(ks[8], (DEPTH, D_MODEL, D_MODEL), f32) * D_MODEL ** -0.5,
        "norm2_g": 1.0 + 0.02 * n(ks[9], (DEPTH, D_MODEL), f32),
        "w_ffn_gate": n(ks[10], (DEPTH, D_MODEL, FFN_HIDDEN), f32) * D_MODEL ** -0.5,
        "w_ffn_up": n(ks[11], (DEPTH, D_MODEL, FFN_HIDDEN), f32) * D_MODEL ** -0.5,
        "w_ffn_down": n(ks[12], (DEPTH, FFN_HIDDEN, D_MODEL), f32) * FFN_HIDDEN ** -0.5,
    }


def reference(x, norm1_g, w_in, q_norm_g, k_norm_g, lb_table, hg_norm_g, w_branch, w_out,
              norm2_g, w_ffn_gate, w_ffn_up, w_ffn_down):
    lower_bounds = jnp.cumsum(jax.nn.softmax(lb_table.astype(jnp.float32), axis=0), axis=0)
    h = x
    for layer in range(DEPTH):
        h = hybrid_layer(h, norm1_g[layer], w_in[layer], q_norm_g[layer], k_norm_g[layer],
                         lower_bounds[layer], hg_norm_g[layer], w_branch[layer], w_out[layer],
                         norm2_g[layer], w_ffn_gate[layer], w_ffn_up[layer], w_ffn_down[layer])
    return h
```

```python
import numpy as np
import concourse.bass as bass
import concourse.mybir as mybir
from concourse.bass_utils import run_bass_kernel_spmd

F32 = mybir.dt.float32
BF16 = mybir.dt.bfloat16
AF = mybir.ActivationFunctionType
ALU = mybir.AluOpType

D = 1024
FF = 2816
INW = 5632
TQ = 512
EPS = 1e-6
NSLOT = 3
NJUNK = 0


class Sched:
    def __init__(self):
        self.engs = ["pe", "act", "dve", "pool", "sp"]
        self.ops = {e: [] for e in self.engs}
        self.count = {}
        self.known = {e: {} for e in self.engs}
        self.last_w = {}
        self.last_r = {}
        self.pending_barrier = {e: {} for e in self.engs}
        self.ranges = {}

    def expand(self, keys):
        out = []
        for k in keys:
            rng = self.ranges.get(k)
            if rng is None:
                out.append(k)
            else:
                off, n = rng
                out.extend(("A", g) for g in range(off // 256, (off + n + 255) // 256))
        return out

    def op(self, eng, fn, r=(), w=(), dma_sem=None):
        r = self.expand(r)
        w = self.expand(w)
        sem = eng if dma_sem is None else dma_sem
        inc = 1 if dma_sem is None else 16
        val = self.count.get(sem, 0) + inc
        self.count[sem] = val
        need = dict(self.pending_barrier[eng])
        self.pending_barrier[eng] = {}
        for k in r:
            for s, v in self.last_w.get(k, {}).items():
                if s == eng and eng == "pe":
                    continue
                need[s] = max(need.get(s, 0), v)
        for k in w:
            for s, v in self.last_w.get(k, {}).items():
                if s == eng and eng == "pe":
                    continue
                need[s] = max(need.get(s, 0), v)
            for s, v in self.last_r.get(k, {}).items():
                if s == eng and eng == "pe":
                    continue
                need[s] = max(need.get(s, 0), v)
        waits = []
        for s, v in need.items():
            if self.known[eng].get(s, 0) < v:
                waits.append((s, v))
                self.known[eng][s] = v
        self.ops[eng].append((waits, fn, sem, inc))
        for k in r:
            d = self.last_r.setdefault(k, {})
            d[sem] = max(d.get(sem, 0), val)
        for k in w:
            d = self.last_w.setdefault(k, {})
            d[sem] = max(d.get(sem, 0), val)
        return val

    def barrier(self):
        snap = dict(self.count)
        for e in self.engs:
            for s, v in snap.items():
                if s == e:
                    continue
                self.pending_barrier[e][s] = max(self.pending_barrier[e].get(s, 0), v)


def build(S, dbg=False):
    NT = S // TQ
    NKB = S // 128
    nc = bass.Bass("TRN2", target_bir_lowering=False)
    dram = lambda n, sh, kind="ExternalInput", dt=F32: nc.dram_tensor(n, sh, dt, kind=kind).ap()
    x_d = dram("x", [S, D])
    n1_d = dram("norm1_g", [1, D])
    win_d = dram("w_in", [D, INW])
    qg_d = dram("q_norm_g", [1, 64])
    kg_d = dram("k_norm_g", [1, 64])
    lb_d = dram("lb_table", [2, 512])
    hgg_d = dram("hg_norm_g", [1, 128])
    wbr_d = dram("w_branch", [D, D])
    wout_d = dram("w_out", [D, D])
    n2_d = dram("norm2_g", [1, D])
    wg_d = dram("w_ffn_gate", [D, FF])
    wu_d = dram("w_ffn_up", [D, FF])
    wd_d = dram("w_ffn_down", [FF, D])
    out_d = dram("out", [S, D], kind="ExternalOutput")
    if dbg:
        dbg_osb = dram("dbg_osb", [NT, 128, 4 * 512], kind="ExternalOutput", dt=BF16)
        dbg_ohg = dram("dbg_ohg", [NT, 128, 4 * 512], kind="ExternalOutput", dt=BF16)
        dbg_h = dram("dbg_h", [S, D], kind="ExternalOutput")

    sch = Sched()
    import contextlib
    es = contextlib.ExitStack()
    with es:
        def sb(name, shape, dt):
            return es.enter_context(nc.sbuf_tensor(name, shape, dt))

        x_sb = sb("x_sb", [128, 4, D], F32)
        g1_bc = sb("g1_bc", [128, D], F32)
        g2_bc = sb("g2_bc", [128, D], F32)
        xnT = sb("xnT", [128, 8, TQ], BF16)
        KT = sb("KT", [128, 4, S], BF16)
        V = sb("V", [128, NKB, 512], BF16)
        S32 = sb("S32", [128, 4, 128], F32)
        Sbf = sb("Sbf", [128, 4, 128], BF16)
        ring = [sb(f"ring{i}", [128, 4096], BF16) for i in range(NSLOT)]
        ident = sb("ident", [128, 128], BF16)
        ones_bf = sb("ones_bf", [128, 128], BF16)
        blk64 = sb("blk64", [128, 128], BF16)
        lincl = sb("lincl", [128, 128], BF16)
        nones_bf = sb("nones_bf", [128, 128], BF16)
        mdiag = sb("mdiag", [128, 128], BF16)
        mdiag2 = sb("mdiag2", [128, 2, 128], BF16)
        triblk = sb("triblk", [128, 128], F32)
        trirev = sb("trirev", [128, 128], F32)
        cols = sb("cols", [128, 16], F32)
        oml_bc = sb("oml_bc", [128, 512], F32)
        ss = sb("ss", [128, 16], F32)
        ARENA_W = 79 * 256
        arena = sb("arena", [128, ARENA_W], F32)

        def aview(off_b, shape, dt):
            esz = 4 if dt == F32 else 2
            n = int(np.prod(shape[1:]))
            assert off_b % 4 == 0
            w0 = off_b // 4
            nb = n * esz
            assert nb % 4 == 0 and off_b + nb <= ARENA_W * 4, (off_b, shape)
            ap = arena[:, w0:w0 + nb // 4]
            if dt != F32:
                ap = ap.bitcast(dt)
            if len(shape) == 3:
                ap = ap.rearrange("p (a b) -> p a b", a=shape[1])
            return ap

        KB = 1024
        QT = aview(0, [128, 4, 512], BF16)
        e_sb = aview(4 * KB, [128, 4, 512], F32)
        sp_sb = aview(12 * KB, [128, 4, 512], BF16)
        spcum = aview(16 * KB, [128, 4, 512], BF16)
        w_sb = aview(20 * KB, [128, 4, 512], BF16)
        mixT = aview(0, [128, 8, 512], BF16)
        gtmp = aview(8 * KB, [128, 4, 512], F32)
        o_sbT = aview(24 * KB, [128, 4, 512], BF16)
        o_hgT = aview(28 * KB, [128, 4, 512], BF16)
        xn_tm = aview(32 * KB, [128, 2, D], BF16)
        R3 = 36 * KB
        qhT = aview(R3, [128, 4, 512], BF16)
        keyT = aview(R3 + 4 * KB, [128, 4, 512], BF16)
        logf = aview(R3 + 8 * KB, [128, 4, 512], F32)
        key_tm = aview(R3 + 16 * KB, [128, 4, 512], BF16)
        v_tm = aview(R3 + 20 * KB, [128, 4, 512], BF16)
        ghT = aview(R3 + 24 * KB, [128, 4, 512], BF16)
        EbT = aview(R3 + 28 * KB, [128, 4, 128], F32)
        EnbT = aview(R3 + 30 * KB, [128, 4, 128], F32)
        Ed = aview(R3 + 32 * KB, [128, 512], F32)
        qeT = aview(R3 + 34 * KB, [128, 4, 128], BF16)
        keT = aview(R3 + 35 * KB, [128, 4, 128], BF16)
        kd = aview(R3 + 36 * KB, [128, 512], BF16)
        attn = aview(R3 + 37 * KB, [128, 4, 128], BF16)
        sq_sb = aview(R3 + 38 * KB, [128, 512], BF16)
        tA = aview(R3 + 39 * KB, [128, 512], F32)
        tB = aview(R3 + 41 * KB, [128, 512], F32)
        assert R3 + 43 * KB <= ARENA_W * 4
        lbtmp = aview(R3, [128, 2, 512], F32)
        lb_bc = aview(R3 + 4 * KB, [128, 512], F32)
        actT = aview(R3, [128, 22, 512], BF16)
        ftmp = aview(R3 + 22 * KB, [128, 2, 512], F32)

        AR = sch.ranges
        for i in range(4):
            AR[f"QT{i}"] = (i * KB, KB)
            AR[f"o_sbT{i}"] = (24 * KB + i * KB, KB)
            AR[f"qhT{i}"] = (R3 + i * KB, KB)
            AR[f"keyT{i}"] = (R3 + 4 * KB + i * KB, KB)
            AR[f"logf{i}"] = (R3 + 8 * KB + i * 2 * KB, 2 * KB)
            AR[f"key_tm{i}"] = (R3 + 16 * KB + i * KB, KB)
            AR[f"v_tm{i}"] = (R3 + 20 * KB + i * KB, KB)
            AR[f"ghT{i}"] = (R3 + 24 * KB + i * KB, KB)
            AR[f"attn{i}"] = (R3 + 37 * KB + i * 256, 256)
        AR["e0"] = (4 * KB, 4 * KB)
        AR["e1"] = (8 * KB, 4 * KB)
        for i in range(2):
            AR[f"sp{i}"] = (12 * KB + i * 2 * KB, 2 * KB)
            AR[f"spcum{i}"] = (16 * KB + i * 2 * KB, 2 * KB)
            AR[f"w{i}"] = (20 * KB + i * 2 * KB, 2 * KB)
            AR[f"ft{i}"] = (R3 + 22 * KB + i * 2 * KB, 2 * KB)
        for i in range(8):
            AR[f"mixT{i}"] = (i * KB, KB)
        AR["g0"] = (8 * KB, 2 * KB)
        AR["g1"] = (10 * KB, 2 * KB)
        AR["o_sbT"] = (24 * KB, 4 * KB)
        AR["o_hgT"] = (28 * KB, 4 * KB)
        AR["xn_tm0"] = (32 * KB, 2 * KB)
        AR["xn_tm1"] = (34 * KB, 2 * KB)
        AR["EbT"] = (R3 + 28 * KB, 2 * KB)
        AR["EnbT"] = (R3 + 30 * KB, 2 * KB)
        AR["Ed"] = (R3 + 32 * KB, 2 * KB)
        AR["qeT"] = (R3 + 34 * KB, KB)
        AR["keT"] = (R3 + 35 * KB, KB)
        AR["kd"] = (R3 + 36 * KB, KB)
        AR["sq_sb"] = (R3 + 38 * KB, KB)
        AR["tA"] = (R3 + 39 * KB, 2 * KB)
        AR["tB"] = (R3 + 41 * KB, 2 * KB)
        for i in range(22):
            AR[f"actT{i}"] = (R3 + i * KB, KB)
        AR["lbt"] = (R3, 4 * KB)
        for k_ in ("lb1", "lb2", "lb3"):
            AR[k_] = (R3 + 4 * KB, 2 * KB)
        pp = [es.enter_context(nc.psum_tensor(f"pp{i}", [128, 1024], F32)) for i in range(4)]

        class BankView:
            def __init__(self, ap):
                self.ap = ap

            def __getitem__(self, k):
                return self.ap[k]
        ps = [BankView(pp[i // 2][:, (i % 2) * 512:(i % 2 + 1) * 512]) for i in range(8)]
        pp3 = [pp[i][:].rearrange("p (a b) -> p a b", a=2) for i in range(4)]
        sem_names = ["pe", "act", "dve", "pool", "s_c", "s_c2", "s_g1", "s_g2", "s_d1", "s_d2", "s_d3"] + [f"s_x{i}" for i in range(4)] + [f"s_o{i}" for i in range(4)] + [f"s_w{i}" for i in range(NSLOT)]
        sems = {n: es.enter_context(nc.semaphore(n)) for n in sem_names}

        PE = lambda fn, r, w: sch.op("pe", fn, r, w)
        ACT = lambda fn, r, w: sch.op("act", fn, r, w)
        DVE = lambda fn, r, w: sch.op("dve", fn, r, w)
        POOL = lambda fn, r, w: sch.op("pool", fn, r, w)

        def mm_group(out, pairs, start=True):
            def fn(e):
                ins = None
                n = len(pairs)
                for i, (l, rr) in enumerate(pairs):
                    ins = e.matmul(out, lhsT=l, rhs=rr, start=(start and i == 0), stop=(i == n - 1),
                                   skip_group_check=True)
                return ins
            return fn

        nconst = [0]

        def cdma(out, in_, q="sp"):
            nconst[0] += 1
            sch.op(q, lambda e: e.dma_start(out=out, in_=in_), r=[], w=["const"],
                   dma_sem=("s_c" if q == "sp" else "s_c2"))

        for tc in range(4):
            sch.op("sp", lambda e, tc=tc: e.dma_start(out=x_sb[:, tc, :], in_=x_d[tc * 128:(tc + 1) * 128, :]),
                   r=[], w=[f"x{tc}"], dma_sem=f"s_x{tc}")
        DVE(lambda e: e.memset(ss[:], 0.0), [], ["ssd"])

        sch.op("sp", lambda e: e.dma_start(out=g1_bc[:], in_=n1_d[0:1, :].partition_broadcast(128)), r=[], w=["g1"],
               dma_sem="s_g1")
        sch.op("sp", lambda e: e.dma_start(out=g2_bc[:], in_=n2_d[0:1, :].partition_broadcast(128)), r=[], w=["g2"],
               dma_sem="s_g2")
        for r_ in range(2):
            nconst[0] += 1
            sch.op("sp", (lambda r_: (lambda e: e.dma_start(out=lbtmp[:, r_, :], in_=lb_d[r_:r_ + 1, :].partition_broadcast(128))))(r_),
                   r=[], w=["const", "lbt"], dma_sem="s_c")
        qg_col = qg_d.rearrange("o k -> k o")
        kg_col = kg_d.rearrange("o k -> k o")
        for half in range(2):
            cdma(cols[half * 64:(half + 1) * 64, 0:1], qg_col[:, 0:1], "pool")
            cdma(cols[half * 64:(half + 1) * 64, 1:2], kg_col[:, 0:1], "pool")
        cdma(cols[:, 2:3], hgg_d.rearrange("o k -> k o")[:, 0:1], "pool")
        lbT = lb_d.rearrange("r (h k) -> k r h", h=4)
        for r_ in range(2):
            for h in range(4):
                cdma(cols[:, 8 + r_ * 4 + h:9 + r_ * 4 + h], lbT[:, r_, h:h + 1], "pool" if h % 2 else "sp")

        def pool_const(t, val, sel=None):
            POOL(lambda e: e.memset(t, val), [], ["pconst"])
            if sel is not None:
                pat, op, cm = sel
                POOL(lambda e: e.affine_select(out=t, in_=t, pattern=pat, compare_op=op, fill=0.0, base=0,
                                               channel_multiplier=cm), ["pconst"], ["pconst"])

        pool_const(ident[:], 1.0, ([[-1, 128]], ALU.is_equal, 1))
        pool_const(ones_bf[:], 1.0)
        pool_const(lincl[:], -1.0, ([[-1, 128]], ALU.is_ge, 1))
        pool_const(nones_bf[:], -1.0)
        pool_const(mdiag[:], 1.0, ([[1, 128]], ALU.is_gt, -1))
        pool_const(mdiag2[:], 1.0, ([[0, 2], [1, 128]], ALU.is_gt, -1))
        pool_const(triblk[:], 1.0, ([[1, 128]], ALU.is_ge, -1))
        POOL(lambda e: e.memset(triblk[0:64, 64:128], 0.0), ["pconst"], ["pconst"])
        pool_const(trirev[:], 1.0, ([[-1, 128]], ALU.is_gt, 1))
        POOL(lambda e: e.memset(trirev[64:128, 0:64], 0.0), ["pconst"], ["pconst"])
        pool_const(blk64[:], 0.0)
        POOL(lambda e: e.memset(blk64[0:64, 0:64], 1.0), ["pconst"], ["pconst"])
        POOL(lambda e: e.memset(blk64[64:128, 64:128], 1.0), ["pconst"], ["pconst"])
        POOL(lambda e: e.memset(S32[:], 0.0), [], [f"S{h}" for h in range(4)])
        POOL(lambda e: e.memset(Sbf[:], 0.0), [], [f"S{h}" for h in range(4)])
        POOL(lambda e: e.memset(cols[:, 3:4], EPS), [], ["pconst"])

        ACT(lambda e: e.activation(out=ss[:, 15:16], in_=ss[:, 15:16], func=AF.Exp), [], ["ssd"])
        def late_consts():
            DVE(lambda e: e.tensor_tensor(out=lbtmp[:, 0, :], in0=lbtmp[:, 0, :], in1=lbtmp[:, 1, :], op=ALU.subtract),
                ["const"], ["lbt"])
            ACT(lambda e: e.activation(out=lb_bc[:], in_=lbtmp[:, 0, :], func=AF.Exp, scale=-1.0), ["lbt"], ["lb1"])
            DVE(lambda e: e.tensor_scalar(out=lb_bc[:], in0=lb_bc[:], scalar1=1.0, scalar2=None, op0=ALU.add),
                ["lb1"], ["lb2"])
            DVE(lambda e: e.reciprocal(out=lb_bc[:], in_=lb_bc[:]), ["lb2"], ["lb3"])
            DVE(lambda e: e.tensor_scalar(out=oml_bc[:], in0=lb_bc[:], scalar1=-1.0, scalar2=1.0, op0=ALU.mult,
                                          op1=ALU.add), ["lb3"], ["const2"])
            DVE(lambda e: e.tensor_tensor(out=cols[:, 8:12], in0=cols[:, 12:16], in1=cols[:, 8:12], op=ALU.subtract),
                ["const"], ["colt"])
            ACT(lambda e: e.activation(out=cols[:, 8:12], in_=cols[:, 8:12], func=AF.Exp, scale=-1.0),
                ["colt"], ["colt2"])
            DVE(lambda e: e.tensor_scalar(out=cols[:, 8:12], in0=cols[:, 8:12], scalar1=1.0, scalar2=None, op0=ALU.add),
                ["colt2"], ["colt3"])
            DVE(lambda e: e.reciprocal(out=cols[:, 4:8], in_=cols[:, 8:12]), ["colt3"], ["const2"])
            DVE(lambda e: e.tensor_scalar(out=cols[:, 0:1], in0=cols[:, 0:1], scalar1=0.125, scalar2=None, op0=ALU.mult),
                ["const"], ["const2"])

        CONST = ["const", "const2", "pconst"]

        def slab_list():
            L = []
            W2 = lambda wd_, c0, n: wd_[:, c0:c0 + n].rearrange("(kc p) n -> p kc n", p=128)
            for j in (3, 4, 5, 6, 0, 1, 2):
                L.append(("in", j, W2(win_d, j * 512, 512), (8, 512)))
            for j in range(2):
                L.append(("br", j, W2(wbr_d, j * 512, 512), (8, 512)))
                L.append(("gs", j, W2(win_d, 3584 + j * 512, 512), (8, 512)))
                L.append(("gh", j, W2(win_d, 4608 + j * 512, 512), (8, 512)))
            for j in range(2):
                L.append(("out", j, W2(wout_d, j * 512, 512), (8, 512)))
            for j in range(6):
                n = 512 if j < 5 else 256
                L.append(("fg", j, W2(wg_d, j * 512, n), (8, n)))
                L.append(("fu", j, W2(wu_d, j * 512, n), (8, n)))
            for p in range(2):
                for (f0, nf) in ((0, 8), (8, 8), (16, 6)):
                    src = wd_d[f0 * 128:(f0 + nf) * 128, p * 512:(p + 1) * 512].rearrange("(kc p) n -> p kc n", p=128)
                    L.append(("fd", (p, f0, nf), src, (nf, 512)))
            return L

        per_tile = slab_list()
        NSL = len(per_tile)
        state = {"issued": 0}
        TOTAL = NSL * NT
        scratch = []
        for k, (_, _, src, (a, b)) in enumerate(per_tile):
            sc = nc.dram_tensor(f"wsc{k}", [128, a * b], BF16).ap()
            scratch.append(sc)
            sems[f"s_cast{k}"] = es.enter_context(nc.semaphore(f"s_cast{k}"))
            sch.op("pool", (lambda v, s_: (lambda e: e.dma_start(out=v, in_=s_)))(
                sc.rearrange("p (a b) -> p a b", a=a), src), r=([f"wsc{k - 2}"] if k >= 2 else []),
                w=[f"wsc{k}"], dma_sem=f"s_cast{k}")

        def issue_until(idx):
            while state["issued"] <= min(idx, TOTAL - 1):
                i = state["issued"]
                k = i % NSL
                _, _, _, (a, b) = per_tile[k]
                slot = i % NSLOT
                view = ring[slot][:, 0:a * b]
                sch.op("sp", (lambda v, s_: (lambda e: e.dma_start(out=v, in_=s_)))(view, scratch[k][:, :]),
                       r=[f"wsc{k}"], w=[f"ring{slot}"], dma_sem=f"s_w{slot}")
                state["issued"] += 1

        cur = {"i": 0}

        def next_group(kinds):
            i0 = cur["i"]
            assert len(kinds) <= NSLOT
            issue_until(i0 + NSLOT - 1)
            outs = []
            for expect in kinds:
                i = cur["i"]
                cur["i"] += 1
                kind, j, _, (a, b) = per_tile[i % NSL]
                assert kind == expect, (kind, expect)
                slot = i % NSLOT
                view = ring[slot][:, 0:a * b].rearrange("p (a b) -> p a b", a=a)
                outs.append((view, f"ring{slot}"))
            return outs

        def next_slab(expect):
            return next_group([expect])[0]

        bank_rr = {"i": 0}

        def tmp_bank(pool=(0, 1, 2, 3, 4, 5, 6, 7)):
            b = pool[bank_rr["i"] % len(pool)]
            bank_rr["i"] += 1
            return b

        def bk(b):
            return f"ps{b}"

        junkv = aview(R3 + 39 * KB, [128, D], F32)

        def rmsnorm_T(ti, gbc, gkey):
            for tc in range(4):
                ACT(lambda e, tc=tc: e.activation(out=junkv, in_=x_sb[:, tc, :], func=AF.Square,
                                                  accum_out=ss[:, tc:tc + 1]), [f"x{tc}"], ["tA", "tB", f"ss{tc}"])
            ACT(lambda e: e.activation(out=ss[:, 4:8], in_=ss[:, 0:4], func=AF.Ln, scale=1.0 / D, bias=EPS),
                [f"ss{c}" for c in range(4)], ["ssl"])
            ACT(lambda e: e.activation(out=ss[:, 8:12], in_=ss[:, 4:8], func=AF.Exp, scale=-0.5), ["ssl"], ["ssr"])
            for tc in range(4):
                xb = xn_tm[:, tc % 2, :]
                xk = f"xn_tm{tc % 2}"
                DVE(lambda e, tc=tc, xb=xb: e.scalar_tensor_tensor(out=xb, in0=x_sb[:, tc, :],
                                                                   scalar=ss[:, 8 + tc:9 + tc], in1=gbc[:],
                                                                   op0=ALU.mult, op1=ALU.mult),
                    [f"x{tc}", "ssr", gkey], [xk])
                b = tmp_bank()
                psb = ps[b][:].bitcast(BF16)

                def tr(e, psb=psb, xb=xb):
                    ins = None
                    for dc in range(8):
                        ins = e.transpose(out=psb[:, dc * 128:(dc + 1) * 128], in_=xb[:, dc * 128:(dc + 1) * 128],
                                          identity=ident[:])
                    return ins
                PE(tr, [xk, "pconst"], [bk(b)])
                ACT(lambda e, tc=tc, psb=psb: e.activation(out=xnT[:, :, tc * 128:(tc + 1) * 128],
                                                           in_=psb.rearrange("p (a b) -> p a b", a=8),
                                                           func=AF.Copy), [bk(b)], ["xnT"])

        def proj_fm(slab, skey, cc, b, rhsT=None, rkey="xnT", nk=8, koff=0):
            rhsT = xnT if rhsT is None else rhsT
            pairs = [(slab[:, koff + kc, cc * 128:(cc + 1) * 128], rhsT[:, kc, :]) for kc in range(nk)]
            PE(mm_group(ps[b][:], pairs), [skey, rkey], [bk(b)])

        def proj_tm(slab, skey, tc, b, ncols=512):
            pairs = [(xnT[:, kc, tc * 128:(tc + 1) * 128], slab[:, kc, 0:ncols]) for kc in range(8)]
            PE(mm_group(ps[b][:, 0:ncols], pairs), [skey, "xnT"], [bk(b)])

        def sigmoid_from_exp(buf, key):
            ACT(lambda e: e.activation(out=buf, in_=buf, func=AF.Ln, bias=1.0), [key], [key])
            ACT(lambda e: e.activation(out=buf, in_=buf, func=AF.Exp, scale=-1.0), [key], [key])

        for ti in range(NT):
            for tc in range(4):
                if ti == 0:
                    break
                sch.op("sp", lambda e, ti=ti, tc=tc: e.dma_start(
                    out=x_sb[:, tc, :], in_=x_d[ti * TQ + tc * 128:ti * TQ + (tc + 1) * 128, :]),
                    r=[], w=[f"x{tc}"], dma_sem=f"s_x{tc}")
            rmsnorm_T(ti, g1_bc, "g1")
            if ti == 0:
                late_consts()

            slab, skey = next_slab("in")
            for h in range(4):
                a = tmp_bank()
                proj_fm(slab, skey, h, a)
                ACT(lambda e, a=a: e.activation(out=tA[:], in_=ps[a][:], func=AF.Exp, scale=-1.0), [bk(a)], ["tA"])
                sigmoid_from_exp(tA[:], "tA")
                DVE(lambda e, a=a, h=h: e.tensor_tensor(out=qhT[:, h, :], in0=ps[a][:], in1=tA[:], op=ALU.mult),
                    [bk(a), "tA"], [f"qhT{h}"])
            slab, skey = next_slab("in")
            for h in range(4):
                a = tmp_bank()
                proj_fm(slab, skey, h, a)
                ACT(lambda e, a=a: e.activation(out=tA[:], in_=ps[a][:], func=AF.Exp, scale=1.0), [bk(a)], ["tA"])
                sigmoid_from_exp(tA[:], "tA")
                DVE(lambda e, h=h: e.tensor_scalar(out=keyT[:, h, :], in0=tA[:], scalar1=cols[:, 4 + h:5 + h],
                                                   scalar2=None, op0=ALU.mult), ["tA"] + CONST, [f"keyT{h}"])
            for tc in range(4):
                a = tmp_bank()
                proj_tm(slab, skey, tc, a)
                ACT(lambda e, a=a: e.activation(out=tB[:], in_=ps[a][:], func=AF.Exp, scale=1.0), [bk(a)], ["tB"])
                sigmoid_from_exp(tB[:], "tB")
                DVE(lambda e: e.tensor_tensor(out=tB[:], in0=tB[:], in1=oml_bc[:], op=ALU.mult), ["tB"] + CONST,
                    ["tB"])
                DVE(lambda e, tc=tc: e.tensor_copy(out=key_tm[:, tc, :], in_=tB[:]), ["tB"], [f"key_tm{tc}"])
                ACT(lambda e, tc=tc: e.activation(out=logf[:, tc, :], in_=tB[:], func=AF.Ln, scale=-1.0, bias=1.0),
                    ["tB"], [f"logf{tc}"])
            slab, skey = next_slab("in")
            for tc in range(4):
                a = tmp_bank()
                proj_tm(slab, skey, tc, a)
                ACT(lambda e, a=a, tc=tc: e.activation(out=v_tm[:, tc, :], in_=ps[a][:], func=AF.Copy),
                    [bk(a)], [f"v_tm{tc}"])
            slab, skey = next_slab("in")
            for h in range(4):
                a = tmp_bank()
                proj_fm(slab, skey, h, a)
                ACT(lambda e, a=a: e.activation(out=tA[:], in_=ps[a][:], func=AF.Exp, scale=-1.0), [bk(a)], ["tA"])
                sigmoid_from_exp(tA[:], "tA")
                DVE(lambda e, a=a, h=h: e.tensor_tensor(out=ghT[:, h, :], in0=ps[a][:], in1=tA[:], op=ALU.mult),
                    [bk(a), "tA"], [f"ghT{h}"])

            qk_items = []
            for which, (slab, skey) in enumerate(next_group(["in", "in"])):
                for hp in range(4):
                    qk_items.append((which, hp, slab, skey))
            sqb = [(sq_sb, "sq_sb"), (kd, "kd")]
            rsb = [(tA, "tA"), (tB, "tB")]
            qk_state = {}

            def qk_a(i):
                which, hp, slab, skey = qk_items[i]
                a = tmp_bank()
                qk_state[i] = a
                proj_fm(slab, skey, hp, a)
                sq, sqk = sqb[i % 2]
                ACT(lambda e, a=a, sq=sq: e.activation(out=sq[:], in_=ps[a][:], func=AF.Square), [bk(a)], [sqk])

            def qk_b(i):
                which, hp, slab, skey = qk_items[i]
                a = qk_state[i]
                sq, sqk = sqb[i % 2]
                rs, rsk = rsb[i % 2]
                b2 = tmp_bank()
                PE(mm_group(ps[b2][:], [(blk64[:], sq[:])]), [sqk] + CONST, [bk(b2)])
                ACT(lambda e, b2=b2, rs=rs: e.activation(out=rs[:], in_=ps[b2][:], func=AF.Ln, scale=1.0 / 64,
                                                         bias=EPS), [bk(b2)] + CONST, [rsk])
                ACT(lambda e, rs=rs: e.activation(out=rs[:], in_=rs[:], func=AF.Exp, scale=-0.5), [rsk], [rsk])
                if which == 0:
                    DVE(lambda e, a=a, hp=hp, rs=rs: e.scalar_tensor_tensor(out=QT[:, hp, :], in0=ps[a][:],
                                                                            scalar=cols[:, 0:1], in1=rs[:],
                                                                            op0=ALU.mult, op1=ALU.mult),
                        [bk(a), rsk] + CONST, [f"QT{hp}"])
                else:
                    DVE(lambda e, a=a, hp=hp, ti=ti, rs=rs: e.scalar_tensor_tensor(
                        out=KT[:, hp, ti * TQ:(ti + 1) * TQ], in0=ps[a][:], scalar=cols[:, 1:2], in1=rs[:],
                        op0=ALU.mult, op1=ALU.mult), [bk(a), rsk] + CONST, [f"KT{hp}.{ti}"])

            qk_a(0)
            for i in range(8):
                if i + 1 < 8:
                    qk_a(i + 1)
                qk_b(i)
            slab, skey = next_slab("in")
            for tc in range(4):
                a = tmp_bank()
                proj_tm(slab, skey, tc, a)
                ACT(lambda e, a=a, tc=tc, ti=ti: e.activation(out=V[:, ti * 4 + tc, :], in_=ps[a][:], func=AF.Copy),
                    [bk(a)], [f"V{ti * 4 + tc}"])
            blocks = [(hp, kb) for hp in range(4) for kb in range(ti * 4 + 3, -1, -1)]
            NB = len(blocks)
            Z3 = pp3[0]
            ZK = [bk(0), bk(1)]
            R3v = [pp3[1], pp3[2]]
            RK = [[bk(2), bk(3)], [bk(4), bk(5)]]

            def c0_of(kb):
                j = kb - ti * 4
                return 128 * j if j >= 0 else 0

            def st1(n):
                hp, kb = blocks[n]
                c0 = c0_of(kb)

                def fn(e):
                    ins = None
                    for hh in range(2):
                        rows = slice(hh * 64, hh * 64 + 64)
                        ins = e.matmul(Z3[:, hh, c0:512], lhsT=KT[rows, hp, kb * 128:(kb + 1) * 128],
                                       rhs=QT[rows, hp, c0:512], start=True, stop=True, skip_group_check=True)
                    return ins
                PE(fn, [f"KT{hp}.{kb // 4}", f"QT{hp}"], ZK)
                p = n % 2
                ACT(lambda e: e.activation(out=e_sb[:, 2 * p:2 * p + 2, c0:512], in_=Z3[:, :, c0:512], func=AF.Exp),
                    ZK, [f"e{p}"])

            def st1b(n):
                hp, kb = blocks[n]
                c0 = c0_of(kb)
                p = n % 2
                ACT(lambda e: e.activation(out=sp_sb[:, 2 * p:2 * p + 2, c0:512], in_=e_sb[:, 2 * p:2 * p + 2, c0:512],
                                           func=AF.Ln, bias=1.0), [f"e{p}"], [f"sp{p}"])
                if kb >= ti * 4:
                    DVE(lambda e: e.tensor_tensor(out=sp_sb[:, 2 * p:2 * p + 2, c0:c0 + 128],
                                                  in0=sp_sb[:, 2 * p:2 * p + 2, c0:c0 + 128], in1=mdiag2[:],
                                                  op=ALU.mult), [f"sp{p}"] + CONST, [f"sp{p}"])

            def st2(n):
                hp, kb = blocks[n]
                c0 = c0_of(kb)
                p = n % 2
                Rv = R3v[p]
                first = (kb == ti * 4 + 3)
                if first:
                    DVE(lambda e: e.memset(spcum[:], 0.0), [], ["spcum0", "spcum1"])
                k_idx = (ti * 4 + 3) - kb
                cp = k_idx % 2

                def fn(e):
                    ins = None
                    for hh in range(2):
                        rows = slice(hh * 64, hh * 64 + 64)
                        ins = e.matmul(Rv[:, hh, c0:512], lhsT=KT[rows, hp, kb * 128:(kb + 1) * 128],
                                       rhs=QT[rows, hp, c0:512], start=True, stop=False, skip_group_check=True)
                    return ins
                PE(fn, [f"KT{hp}.{kb // 4}", f"QT{hp}"], RK[p])
                rk = [f"sp{p}"] + CONST + ([] if first else [f"spcum{cp}"])

                def fn2(e):
                    ins = None
                    for hh in range(2):
                        ins = e.matmul(Rv[:, hh, c0:512], lhsT=lincl[:], rhs=sp_sb[:, 2 * p + hh, c0:512], start=False,
                                       stop=first, skip_group_check=True)
                        if not first:
                            ins = e.matmul(Rv[:, hh, c0:512], lhsT=nones_bf[:], rhs=spcum[:, 2 * cp + hh, c0:512],
                                           start=False, stop=True, skip_group_check=True)
                    return ins
                PE(fn2, rk, RK[p])
                if kb > 0:
                    DVE(lambda e: e.tensor_tensor(out=spcum[:, 2 * (1 - cp):2 * (1 - cp) + 2, c0:512],
                                                  in0=spcum[:, 2 * cp:2 * cp + 2, c0:512],
                                                  in1=sp_sb[:, 2 * p:2 * p + 2, c0:512], op=ALU.add),
                        [f"spcum{cp}", f"sp{p}"], [f"spcum{1 - cp}"])

            def st2b(n):
                hp, kb = blocks[n]
                c0 = c0_of(kb)
                p = n % 2
                Rv = R3v[p]
                ACT(lambda e: e.activation(out=w_sb[:, 2 * p:2 * p + 2, c0:512], in_=Rv[:, :, c0:512], func=AF.Exp),
                    RK[p], [f"w{p}"])
                if kb >= ti * 4:
                    DVE(lambda e: e.tensor_tensor(out=w_sb[:, 2 * p:2 * p + 2, c0:c0 + 128],
                                                  in0=w_sb[:, 2 * p:2 * p + 2, c0:c0 + 128], in1=mdiag2[:],
                                                  op=ALU.mult), [f"w{p}"] + CONST, [f"w{p}"])

            def st3(n):
                hp, kb = blocks[n]
                c0 = c0_of(kb)
                p = n % 2
                ob = 6 + (hp % 2)
                first = (kb == ti * 4 + 3)

                def fn(e):
                    ins = None
                    for hh in range(2):
                        h = 2 * hp + hh
                        ins = e.matmul(ps[ob][hh * 64:hh * 64 + 64, c0:512], lhsT=V[:, kb, h * 64:(h + 1) * 64],
                                       rhs=w_sb[:, 2 * p + hh, c0:512], start=first, stop=(kb == 0),
                                       skip_group_check=True)
                    return ins
                PE(fn, [f"V{kb}", f"w{p}"], [bk(ob)])
                if kb == 0:
                    ACT(lambda e: e.activation(out=o_sbT[:, hp, :], in_=ps[ob][:], func=AF.Copy),
                        [bk(ob)], [f"o_sbT{hp}"])

            for i in range(NB + 3):
                if i < NB:
                    st1(i)
                if 0 <= i - 2 < NB:
                    st2b(i - 2)
                if i < NB:
                    st1b(i)
                if 0 <= i - 1 < NB:
                    st2(i - 1)
                if 0 <= i - 3 < NB:
                    st3(i - 3)

            for tc in range(4):
                tsl = slice(tc * 128, (tc + 1) * 128)
                bB, bD, bA, bO = 0, 1, 2, 3
                def fnb(e, tc=tc):
                    ins = None
                    for h in range(4):
                        ins = e.matmul(ps[bB][:, h * 128:(h + 1) * 128], lhsT=logf[:, tc, h * 128:(h + 1) * 128],
                                       rhs=triblk[:], start=True, stop=True, skip_group_check=True)
                    return ins
                PE(fnb, [f"logf{tc}"] + CONST, [bk(bB)])
                PE(mm_group(ps[bD][:], [(trirev[:], logf[:, tc, :])]), [f"logf{tc}"] + CONST, [bk(bD)])
                ACT(lambda e: e.activation(out=EbT[:], in_=ps[bB][:].rearrange("p (a b) -> p a b", a=4), func=AF.Exp),
                    [bk(bB)], ["EbT"])
                ACT(lambda e: e.activation(out=EnbT[:], in_=ps[bB][:].rearrange("p (a b) -> p a b", a=4),
                                           func=AF.Exp, scale=-1.0), [bk(bB)], ["EnbT"])
                ACT(lambda e: e.activation(out=Ed[:], in_=ps[bD][:], func=AF.Exp), [bk(bD)], ["Ed"])
                DVE(lambda e, tsl=tsl: e.tensor_tensor(out=qeT[:], in0=qhT[:, :, tsl], in1=EbT[:], op=ALU.mult),
                    ["EbT"] + [f"qhT{h}" for h in range(4)], ["qeT"])
                DVE(lambda e, tsl=tsl: e.tensor_tensor(out=keT[:], in0=keyT[:, :, tsl], in1=EnbT[:], op=ALU.mult),
                    ["EnbT"] + [f"keyT{h}" for h in range(4)], ["keT"])
                DVE(lambda e, tc=tc: e.tensor_tensor(out=kd[:], in0=key_tm[:, tc, :], in1=Ed[:], op=ALU.mult),
                    ["Ed", f"key_tm{tc}"], ["kd"])
                def fna(e):
                    ins = None
                    for h in range(4):
                        ins = e.matmul(ps[bA][:, h * 128:(h + 1) * 128], lhsT=keT[:, h, :], rhs=qeT[:, h, :],
                                       start=True, stop=True, skip_group_check=True)
                    return ins
                PE(fna, ["keT", "qeT"], [bk(bA)])
                for h in range(4):
                    DVE(lambda e, h=h: e.tensor_tensor(out=attn[:, h, :], in0=ps[bA][:, h * 128:(h + 1) * 128],
                                                       in1=triblk[:], op=ALU.mult), [bk(bA)] + CONST, [f"attn{h}"])
                for c in range(2):
                    for h in range(4):
                        hs = slice(h * 128, (h + 1) * 128)
                        csl = slice(c * 64, (c + 1) * 64)
                        osl = slice(h * 128 + c * 64, h * 128 + (c + 1) * 64)
                        PE(mm_group(ps[bO][:, osl], [(v_tm[:, tc, hs], attn[:, h, csl]), (Sbf[:, h, :], qeT[:, h, csl])]),
                           [f"v_tm{tc}", f"attn{h}", f"S{h}", "qeT"], [bk(bO)])
                        sbk = 4 + h
                        PE(mm_group(ps[sbk][:, 0:128], [(kd[csl, hs], v_tm[csl, tc, hs])]), ["kd", f"v_tm{tc}"],
                           [bk(sbk)])
                        last = c * 64 + 63
                        DVE(lambda e, h=h, sbk=sbk, last=last: e.scalar_tensor_tensor(
                            out=S32[:, h, :], in0=S32[:, h, :], scalar=EbT[:, h, last:last + 1], in1=ps[sbk][:, 0:128],
                            op0=ALU.mult, op1=ALU.add), [f"S{h}", "EbT", bk(sbk)], [f"S{h}"])
                        DVE(lambda e, h=h: e.tensor_copy(out=Sbf[:, h, :], in_=S32[:, h, :]), [f"S{h}"], [f"S{h}"])
                ACT(lambda e: e.activation(out=sq_sb[:], in_=ps[bO][:], func=AF.Square), [bk(bO)], ["sq_sb"])
                PE(mm_group(ps[bD][:], [(ones_bf[:], sq_sb[:])]), ["sq_sb"] + CONST, [bk(bD)])
                ACT(lambda e: e.activation(out=tA[:], in_=ps[bD][:], func=AF.Ln, scale=1.0 / 128, bias=EPS),
                    [bk(bD)] + CONST, ["tA"])
                ACT(lambda e: e.activation(out=tA[:], in_=tA[:], func=AF.Exp, scale=-0.5), ["tA"], ["tA"])
                DVE(lambda e: e.scalar_tensor_tensor(out=tB[:], in0=ps[bO][:], scalar=cols[:, 2:3], in1=tA[:],
                                                     op0=ALU.mult, op1=ALU.mult), [bk(bO), "tA"] + CONST, ["tB"])
                DVE(lambda e, tsl=tsl: e.tensor_tensor(out=o_hgT[:, :, tsl],
                                                       in0=tB[:].rearrange("p (a b) -> p a b", a=4),
                                                       in1=ghT[:, :, tsl], op=ALU.mult),
                    ["tB"] + [f"ghT{h}" for h in range(4)], ["o_hgT"])

            if dbg:
                sch.op("sp", lambda e, ti=ti: e.dma_start(out=dbg_osb[ti], in_=o_sbT.rearrange("p a b -> p (a b)")),
                       r=[f"o_sbT{hp}" for hp in range(4)], w=[], dma_sem="s_d1")
                sch.op("sp", lambda e, ti=ti: e.dma_start(out=dbg_ohg[ti], in_=o_hgT.rearrange("p a b -> p (a b)")),
                       r=["o_hgT"], w=[], dma_sem="s_d2")

            for j in range(2):
                (sl_br, k_br), (sl_gs, k_gs), (sl_gh, k_gh) = next_group(["br", "gs", "gh"])
                for cc in range(4):
                    dcc = j * 4 + cc
                    b_ys, b_yh, b_gs, b_gh = 0, 1, 2, 3
                    if dcc % 2 == 1:
                        b_ys, b_yh, b_gs, b_gh = 4, 5, 6, 7
                    proj_fm(sl_br, k_br, cc, b_ys, rhsT=o_sbT, rkey="o_sbT", nk=4, koff=0)
                    proj_fm(sl_br, k_br, cc, b_yh, rhsT=o_hgT, rkey="o_hgT", nk=4, koff=4)
                    proj_fm(sl_gs, k_gs, cc, b_gs)
                    proj_fm(sl_gh, k_gh, cc, b_gh)
                    g0, g1 = gtmp[:, 0, :], gtmp[:, 1, :]
                    ACT(lambda e, b=b_gs, g0=g0: e.activation(out=g0, in_=ps[b][:], func=AF.Exp, scale=-1.0),
                        [bk(b_gs)], ["g0"])
                    ACT(lambda e, b=b_gh, g1=g1: e.activation(out=g1, in_=ps[b][:], func=AF.Exp, scale=-1.0),
                        [bk(b_gh)], ["g1"])
                    sigmoid_from_exp(g0, "g0")
                    sigmoid_from_exp(g1, "g1")
                    DVE(lambda e, b=b_ys, g0=g0: e.tensor_tensor(out=g0, in0=ps[b][:], in1=g0, op=ALU.mult),
                        [bk(b_ys), "g0"], ["g0"])
                    DVE(lambda e, b=b_yh, g1=g1: e.tensor_tensor(out=g1, in0=ps[b][:], in1=g1, op=ALU.mult),
                        [bk(b_yh), "g1"], ["g1"])
                    DVE(lambda e, dcc=dcc, g0=g0, g1=g1: e.tensor_tensor(out=mixT[:, dcc, :], in0=g0, in1=g1,
                                                                         op=ALU.add), ["g0", "g1"], [f"mixT{dcc}"])
            (sl_o0, k_o0), (sl_o1, k_o1) = next_group(["out", "out"])
            for tc in range(4):
                for j, (sl_o, k_o) in enumerate(((sl_o0, k_o0), (sl_o1, k_o1))):
                    b = tmp_bank((0, 1, 2, 3, 4, 5, 6, 7))
                    pairs = [(mixT[:, dcc, tc * 128:(tc + 1) * 128], sl_o[:, dcc, :]) for dcc in range(8)]
                    PE(mm_group(ps[b][:], pairs), [k_o] + [f"mixT{d_}" for d_ in range(8)], [bk(b)])
                    DVE(lambda e, b=b, tc=tc, j=j: e.tensor_tensor(out=x_sb[:, tc, j * 512:(j + 1) * 512],
                                                                   in0=ps[b][:], in1=x_sb[:, tc, j * 512:(j + 1) * 512],
                                                                   op=ALU.add), [bk(b), f"x{tc}"], [f"x{tc}"])
            if dbg:
                sch.op("sp", lambda e, ti=ti: e.dma_start(
                    out=dbg_h[ti * TQ:(ti + 1) * TQ, :].rearrange("(tc p) d -> p tc d", p=128), in_=x_sb[:]),
                    r=[f"x{tc}" for tc in range(4)], w=[], dma_sem="s_d3")

            rmsnorm_T(ti, g2_bc, "g2")
            for j in range(6):
                (sl_g, k_g), (sl_u, k_u) = next_group(["fg", "fu"])
                ncc = 4 if j < 5 else 2
                for cc in range(ncc):
                    fc = j * 4 + cc
                    bg = tmp_bank((0, 1, 2, 3, 4, 5, 6, 7))
                    bu = tmp_bank((0, 1, 2, 3, 4, 5, 6, 7))
                    proj_fm(sl_g, k_g, cc, bg)
                    proj_fm(sl_u, k_u, cc, bu)
                    f0 = ftmp[:, fc % 2, :]
                    fk = f"ft{fc % 2}"
                    ACT(lambda e, bg=bg, f0=f0: e.activation(out=f0, in_=ps[bg][:], func=AF.Exp, scale=-1.0),
                        [bk(bg)], [fk])
                    sigmoid_from_exp(f0, fk)
                    DVE(lambda e, bg=bg, f0=f0: e.tensor_tensor(out=f0, in0=ps[bg][:], in1=f0, op=ALU.mult),
                        [bk(bg), fk], [fk])
                    DVE(lambda e, bu=bu, f0=f0, fc=fc: e.tensor_tensor(out=actT[:, fc, :], in0=ps[bu][:], in1=f0,
                                                                       op=ALU.mult), [bk(bu), fk], [f"actT{fc}"])
            for p in range(2):
                banks = (0, 1, 2, 3) if p == 0 else (4, 5, 6, 7)
                for gi, (f0_, nf) in enumerate(((0, 8), (8, 8), (16, 6))):
                    sl_d, k_d = next_slab("fd")
                    for tc in range(4):
                        b = banks[tc]
                        pairs = [(actT[:, f0_ + k, tc * 128:(tc + 1) * 128], sl_d[:, k, :]) for k in range(nf)]

                        def fn(e, pairs=pairs, b=b, gi=gi):
                            ins = None
                            for i, (l, rr) in enumerate(pairs):
                                ins = e.matmul(ps[b][:], lhsT=l, rhs=rr, start=(gi == 0 and i == 0),
                                               stop=(gi == 2 and i == len(pairs) - 1), skip_group_check=True)
                            return ins
                        PE(fn, [k_d] + [f"actT{f0_ + k}" for k in range(nf)], [bk(b)])
                for tc in range(4):
                    b = banks[tc]
                    DVE(lambda e, b=b, tc=tc, p=p: e.tensor_tensor(out=x_sb[:, tc, p * 512:(p + 1) * 512],
                                                                   in0=ps[b][:], in1=x_sb[:, tc, p * 512:(p + 1) * 512],
                                                                   op=ALU.add), [bk(b), f"x{tc}"], [f"x{tc}"])
            for tc in range(4):
                sch.op("sp", lambda e, ti=ti, tc=tc: e.dma_start(
                    out=out_d[ti * TQ + tc * 128:ti * TQ + (tc + 1) * 128, :], in_=x_sb[:, tc, :]),
                    r=[f"x{tc}"], w=[], dma_sem=f"s_o{tc}")

        final_o = [sch.count.get(f"s_o{i}", 0) for i in range(4)]

        with nc.Block() as block:
            def emit(e, name):
                for waits, fn, sem, inc in sch.ops[name]:
                    for s, v in waits:
                        e.wait_ge(sems[s], v)
                    ins = fn(e)
                    ins.then_inc(sems[sem], inc)

            @block.sync
            def _(e):
                emit(e, "sp")
                for i in range(4):
                    e.wait_ge(sems[f"s_o{i}"], final_o[i])

            @block.gpsimd
            def _(e):
                emit(e, "pool")

            @block.scalar
            def _(e):
                emit(e, "act")

            @block.vector
            def _(e):
                emit(e, "dve")

            @block.tensor
            def _(e):
                emit(e, "pe")
    return nc


_CACHE = {}


def run(inputs, S, n_cores, dbg=False):
    key = (S, dbg)
    if key not in _CACHE:
        _CACHE[key] = build(S, dbg)
    nc = _CACHE[key]
    f = lambda a: np.ascontiguousarray(np.asarray(a, dtype=np.float32))
    shared = {
        "norm1_g": f(inputs["norm1_g"]).reshape(1, D),
        "w_in": f(inputs["w_in"]).reshape(D, INW),
        "q_norm_g": f(inputs["q_norm_g"]).reshape(1, 64),
        "k_norm_g": f(inputs["k_norm_g"]).reshape(1, 64),
        "lb_table": f(inputs["lb_table"]).reshape(2, 512),
        "hg_norm_g": f(inputs["hg_norm_g"]).reshape(1, 128),
        "w_branch": f(inputs["w_branch"]).reshape(D, D),
        "w_out": f(inputs["w_out"]).reshape(D, D),
        "norm2_g": f(inputs["norm2_g"]).reshape(1, D),
        "w_ffn_gate": f(inputs["w_ffn_gate"]).reshape(D, FF),
        "w_ffn_up": f(inputs["w_ffn_up"]).reshape(D, FF),
        "w_ffn_down": f(inputs["w_ffn_down"]).reshape(FF, D),
    }
    x = f(inputs["x"])
    in_maps = [dict(shared, x=x[i]) for i in range(n_cores)]
    res = run_bass_kernel_spmd(nc, in_maps, core_ids=list(range(n_cores)))
    return res.results


def kernel(x, norm1_g, w_in, q_norm_g, k_norm_g, lb_table, hg_norm_g, w_branch, w_out, norm2_g,
           w_ffn_gate, w_ffn_up, w_ffn_down):
    inputs = dict(x=x, norm1_g=norm1_g, w_in=w_in, q_norm_g=q_norm_g, k_norm_g=k_norm_g, lb_table=lb_table,
                  hg_norm_g=hg_norm_g, w_branch=w_branch, w_out=w_out, norm2_g=norm2_g, w_ffn_gate=w_ffn_gate,
                  w_ffn_up=w_ffn_up, w_ffn_down=w_ffn_down)
    B, S, _ = np.asarray(x).shape
    res = run(inputs, S, B)
    return np.stack([r["out"] for r in res], axis=0).astype(np.float32)
```

```python
import numpy as np
import concourse.bass as bass
import concourse.mybir as mybir
from concourse.bass_utils import run_bass_kernel_spmd

F32 = mybir.dt.float32
BF16 = mybir.dt.bfloat16
AF = mybir.ActivationFunctionType
ALU = mybir.AluOpType

D = 1024
FF = 2816
INW = 5632
TQ = 512
EPS = 1e-6
NSLOT = 3
NJUNK = 0


class Sched:
    def __init__(self):
        self.engs = ["pe", "act", "dve", "pool", "sp"]
        self.ops = {e: [] for e in self.engs}
        self.count = {}
        self.known = {e: {} for e in self.engs}
        self.last_w = {}
        self.last_r = {}
        self.pending_barrier = {e: {} for e in self.engs}
        self.ranges = {}

    def expand(self, keys):
        out = []
        for k in keys:
            rng = self.ranges.get(k)
            if rng is None:
                out.append(k)
            else:
                off, n = rng
                out.extend(("A", g) for g in range(off // 256, (off + n + 255) // 256))
        return out

    def op(self, eng, fn, r=(), w=(), dma_sem=None):
        r = self.expand(r)
        w = self.expand(w)
        sem = eng if dma_sem is None else dma_sem
        inc = 1 if dma_sem is None else 16
        val = self.count.get(sem, 0) + inc
        self.count[sem] = val
        need = dict(self.pending_barrier[eng])
        self.pending_barrier[eng] = {}
        for k in r:
            for s, v in self.last_w.get(k, {}).items():
                if s == eng and eng == "pe":
                    continue
                need[s] = max(need.get(s, 0), v)
        for k in w:
            for s, v in self.last_w.get(k, {}).items():
                if s == eng and eng == "pe":
                    continue
                need[s] = max(need.get(s, 0), v)
            for s, v in self.last_r.get(k, {}).items():
                if s == eng and eng == "pe":
                    continue
                need[s] = max(need.get(s, 0), v)
        waits = []
        for s, v in need.items():
            if self.known[eng].get(s, 0) < v:
                waits.append((s, v))
                self.known[eng][s] = v
        self.ops[eng].append((waits, fn, sem, inc))
        for k in r:
            d = self.last_r.setdefault(k, {})
            d[sem] = max(d.get(sem, 0), val)
        for k in w:
            d = self.last_w.setdefault(k, {})
            d[sem] = max(d.get(sem, 0), val)
        return val

    def barrier(self):
        snap = dict(self.count)
        for e in self.engs:
            for s, v in snap.items():
                if s == e:
                    continue
                self.pending_barrier[e][s] = max(self.pending_barrier[e].get(s, 0), v)


def build(S, dbg=False):
    NT = S // TQ
    NKB = S // 128
    nc = bass.Bass("TRN2", target_bir_lowering=False)
    dram = lambda n, sh, kind="ExternalInput", dt=F32: nc.dram_tensor(n, sh, dt, kind=kind).ap()
    x_d = dram("x", [S, D])
    n1_d = dram("norm1_g", [1, D])
    win_d = dram("w_in", [D, INW])
    qg_d = dram("q_norm_g", [1, 64])
    kg_d = dram("k_norm_g", [1, 64])
    lb_d = dram("lb_table", [2, 512])
    hgg_d = dram("hg_norm_g", [1, 128])
    wbr_d = dram("w_branch", [D, D])
    wout_d = dram("w_out", [D, D])
    n2_d = dram("norm2_g", [1, D])
    wg_d = dram("w_ffn_gate", [D, FF])
    wu_d = dram("w_ffn_up", [D, FF])
    wd_d = dram("w_ffn_down", [FF, D])
    out_d = dram("out", [S, D], kind="ExternalOutput")
    if dbg:
        dbg_osb = dram("dbg_osb", [NT, 128, 4 * 512], kind="ExternalOutput", dt=BF16)
        dbg_ohg = dram("dbg_ohg", [NT, 128, 4 * 512], kind="ExternalOutput", dt=BF16)
        dbg_h = dram("dbg_h", [S, D], kind="ExternalOutput")

    sch = Sched()
    import contextlib
    es = contextlib.ExitStack()
    with es:
        def sb(name, shape, dt):
            return es.enter_context(nc.sbuf_tensor(name, shape, dt))

        x_sb = sb("x_sb", [128, 4, D], F32)
        g1_bc = sb("g1_bc", [128, D], F32)
        g2_bc = sb("g2_bc", [128, D], F32)
        xnT = sb("xnT", [128, 8, TQ], BF16)
        KT = sb("KT", [128, 4, S], BF16)
        V = sb("V", [128, NKB, 512], BF16)
        S32 = sb("S32", [128, 4, 128], F32)
        Sbf = sb("Sbf", [128, 4, 128], BF16)
        ring = [sb(f"ring{i}", [128, 4096], BF16) for i in range(NSLOT)]
        ident = sb("ident", [128, 128], BF16)
        ones_bf = sb("ones_bf", [128, 128], BF16)
        blk64 = sb("blk64", [128, 128], BF16)
        lincl = sb("lincl", [128, 128], BF16)
        nones_bf = sb("nones_bf", [128, 128], BF16)
        mdiag = sb("mdiag", [128, 128], BF16)
        mdiag2 = sb("mdiag2", [128, 2, 128], BF16)
        triblk = sb("triblk", [128, 128], F32)
        trirev = sb("trirev", [128, 128], F32)
        cols = sb("cols", [128, 16], F32)
        oml_bc = sb("oml_bc", [128, 512], F32)
        ss = sb("ss", [128, 16], F32)
        ARENA_W = 79 * 256
        arena = sb("arena", [128, ARENA_W], F32)

        def aview(off_b, shape, dt):
            esz = 4 if dt == F32 else 2
            n = int(np.prod(shape[1:]))
            assert off_b % 4 == 0
            w0 = off_b // 4
            nb = n * esz
            assert nb % 4 == 0 and off_b + nb <= ARENA_W * 4, (off_b, shape)
            ap = arena[:, w0:w0 + nb // 4]
            if dt != F32:
                ap = ap.bitcast(dt)
            if len(shape) == 3:
                ap = ap.rearrange("p (a b) -> p a b", a=shape[1])
            return ap

        KB = 1024
        QT = aview(0, [128, 4, 512], BF16)
        e_sb = aview(4 * KB, [128, 4, 512], F32)
        sp_sb = aview(12 * KB, [128, 4, 512], BF16)
        spcum = aview(16 * KB, [128, 4, 512], BF16)
        w_sb = aview(20 * KB, [128, 4, 512], BF16)
        mixT = aview(0, [128, 8, 512], BF16)
        gtmp = aview(8 * KB, [128, 4, 512], F32)
        o_sbT = aview(24 * KB, [128, 4, 512], BF16)
        o_hgT = aview(28 * KB, [128, 4, 512], BF16)
        xn_tm = aview(32 * KB, [128, 2, D], BF16)
        R3 = 36 * KB
        qhT = aview(R3, [128, 4, 512], BF16)
        keyT = aview(R3 + 4 * KB, [128, 4, 512], BF16)
        logf = aview(R3 + 8 * KB, [128, 4, 512], F32)
        key_tm = aview(R3 + 16 * KB, [128, 4, 512], BF16)
        v_tm = aview(R3 + 20 * KB, [128, 4, 512], BF16)
        ghT = aview(R3 + 24 * KB, [128, 4, 512], BF16)
        EbT = aview(R3 + 28 * KB, [128, 4, 128], F32)
        EnbT = aview(R3 + 30 * KB, [128, 4, 128], F32)
        Ed = aview(R3 + 32 * KB, [128, 512], F32)
        qeT = aview(R3 + 34 * KB, [128, 4, 128], BF16)
        keT = aview(R3 + 35 * KB, [128, 4, 128], BF16)
        kd = aview(R3 + 36 * KB, [128, 512], BF16)
        attn = aview(R3 + 37 * KB, [128, 4, 128], BF16)
        sq_sb = aview(R3 + 38 * KB, [128, 512], BF16)
        tA = aview(R3 + 39 * KB, [128, 512], F32)
        tB = aview(R3 + 41 * KB, [128, 512], F32)
        assert R3 + 43 * KB <= ARENA_W * 4
        lbtmp = aview(R3, [128, 2, 512], F32)
        lb_bc = aview(R3 + 4 * KB, [128, 512], F32)
        actT = aview(R3, [128, 22, 512], BF16)
        ftmp = aview(R3 + 22 * KB, [128, 2, 512], F32)

        AR = sch.ranges
        for i in range(4):
            AR[f"QT{i}"] = (i * KB, KB)
            AR[f"o_sbT{i}"] = (24 * KB + i * KB, KB)
            AR[f"qhT{i}"] = (R3 + i * KB, KB)
            AR[f"keyT{i}"] = (R3 + 4 * KB + i * KB, KB)
            AR[f"logf{i}"] = (R3 + 8 * KB + i * 2 * KB, 2 * KB)
            AR[f"key_tm{i}"] = (R3 + 16 * KB + i * KB, KB)
            AR[f"v_tm{i}"] = (R3 + 20 * KB + i * KB, KB)
            AR[f"ghT{i}"] = (R3 + 24 * KB + i * KB, KB)
            AR[f"attn{i}"] = (R3 + 37 * KB + i * 256, 256)
        AR["e0"] = (4 * KB, 4 * KB)
        AR["e1"] = (8 * KB, 4 * KB)
        for i in range(2):
            AR[f"sp{i}"] = (12 * KB + i * 2 * KB, 2 * KB)
            AR[f"spcum{i}"] = (16 * KB + i * 2 * KB, 2 * KB)
            AR[f"w{i}"] = (20 * KB + i * 2 * KB, 2 * KB)
            AR[f"ft{i}"] = (R3 + 22 * KB + i * 2 * KB, 2 * KB)
        for i in range(8):
            AR[f"mixT{i}"] = (i * KB, KB)
        AR["g0"] = (8 * KB, 2 * KB)
        AR["g1"] = (10 * KB, 2 * KB)
        AR["o_sbT"] = (24 * KB, 4 * KB)
        AR["o_hgT"] = (28 * KB, 4 * KB)
        AR["xn_tm0"] = (32 * KB, 2 * KB)
        AR["xn_tm1"] = (34 * KB, 2 * KB)
        AR["EbT"] = (R3 + 28 * KB, 2 * KB)
        AR["EnbT"] = (R3 + 30 * KB, 2 * KB)
        AR["Ed"] = (R3 + 32 * KB, 2 * KB)
        AR["qeT"] = (R3 + 34 * KB, KB)
        AR["keT"] = (R3 + 35 * KB, KB)
        AR["kd"] = (R3 + 36 * KB, KB)
        AR["sq_sb"] = (R3 + 38 * KB, KB)
        AR["tA"] = (R3 + 39 * KB, 2 * KB)
        AR["tB"] = (R3 + 41 * KB, 2 * KB)
        for i in range(22):
            AR[f"actT{i}"] = (R3 + i * KB, KB)
        AR["lbt"] = (R3, 4 * KB)
        for k_ in ("lb1", "lb2", "lb3"):
            AR[k_] = (R3 + 4 * KB, 2 * KB)
        pp = [es.enter_context(nc.psum_tensor(f"pp{i}", [128, 1024], F32)) for i in range(4)]

        class BankView:
            def __init__(self, ap):
                self.ap = ap

            def __getitem__(self, k):
                return self.ap[k]
        ps = [BankView(pp[i // 2][:, (i % 2) * 512:(i % 2 + 1) * 512]) for i in range(8)]
        pp3 = [pp[i][:].rearrange("p (a b) -> p a b", a=2) for i in range(4)]
        sem_names = ["pe", "act", "dve", "pool", "s_c", "s_c2", "s_g1", "s_g2", "s_d1", "s_d2", "s_d3"] + [f"s_x{i}" for i in range(4)] + [f"s_o{i}" for i in range(4)] + [f"s_w{i}" for i in range(NSLOT)]
        sems = {n: es.enter_context(nc.semaphore(n)) for n in sem_names}

        PE = lambda fn, r, w: sch.op("pe", fn, r, w)
        ACT = lambda fn, r, w: sch.op("act", fn, r, w)
        DVE = lambda fn, r, w: sch.op("dve", fn, r, w)
        POOL = lambda fn, r, w: sch.op("pool", fn, r, w)

        def mm_group(out, pairs, start=True):
            def fn(e):
                ins = None
                n = len(pairs)
                for i, (l, rr) in enumerate(pairs):
                    ins = e.matmul(out, lhsT=l, rhs=rr, start=(start and i == 0), stop=(i == n - 1),
                                   skip_group_check=True)
                return ins
            return fn

        nconst = [0]

        def cdma(out, in_, q="sp"):
            nconst[0] += 1
            sch.op(q, lambda e: e.dma_start(out=out, in_=in_), r=[], w=["const"],
                   dma_sem=("s_c" if q == "sp" else "s_c2"))

        for tc in range(4):
            sch.op("sp", lambda e, tc=tc: e.dma_start(out=x_sb[:, tc, :], in_=x_d[tc * 128:(tc + 1) * 128, :]),
                   r=[], w=[f"x{tc}"], dma_sem=f"s_x{tc}")
        DVE(lambda e: e.memset(ss[:], 0.0), [], ["ssd"])

        sch.op("sp", lambda e: e.dma_start(out=g1_bc[:], in_=n1_d[0:1, :].partition_broadcast(128)), r=[], w=["g1"],
               dma_sem="s_g1")
        sch.op("sp", lambda e: e.dma_start(out=g2_bc[:], in_=n2_d[0:1, :].partition_broadcast(128)), r=[], w=["g2"],
               dma_sem="s_g2")
        for r_ in range(2):
            nconst[0] += 1
            sch.op("sp", (lambda r_: (lambda e: e.dma_start(out=lbtmp[:, r_, :], in_=lb_d[r_:r_ + 1, :].partition_broadcast(128))))(r_),
                   r=[], w=["const", "lbt"], dma_sem="s_c")
        qg_col = qg_d.rearrange("o k -> k o")
        kg_col = kg_d.rearrange("o k -> k o")
        for half in range(2):
            cdma(cols[half * 64:(half + 1) * 64, 0:1], qg_col[:, 0:1], "pool")
            cdma(cols[half * 64:(half + 1) * 64, 1:2], kg_col[:, 0:1], "pool")
        cdma(cols[:, 2:3], hgg_d.rearrange("o k -> k o")[:, 0:1], "pool")
        lbT = lb_d.rearrange("r (h k) -> k r h", h=4)
        for r_ in range(2):
            for h in range(4):
                cdma(cols[:, 8 + r_ * 4 + h:9 + r_ * 4 + h], lbT[:, r_, h:h + 1], "pool" if h % 2 else "sp")

        def pool_const(t, val, sel=None):
            POOL(lambda e: e.memset(t, val), [], ["pconst"])
            if sel is not None:
                pat, op, cm = sel
                POOL(lambda e: e.affine_select(out=t, in_=t, pattern=pat, compare_op=op, fill=0.0, base=0,
                                               channel_multiplier=cm), ["pconst"], ["pconst"])

        pool_const(ident[:], 1.0, ([[-1, 128]], ALU.is_equal, 1))
        pool_const(ones_bf[:], 1.0)
        pool_const(lincl[:], -1.0, ([[-1, 128]], ALU.is_ge, 1))
        pool_const(nones_bf[:], -1.0)
        pool_const(mdiag[:], 1.0, ([[1, 128]], ALU.is_gt, -1))
        pool_const(mdiag2[:], 1.0, ([[0, 2], [1, 128]], ALU.is_gt, -1))
        pool_const(triblk[:], 1.0, ([[1, 128]], ALU.is_ge, -1))
        POOL(lambda e: e.memset(triblk[0:64, 64:128], 0.0), ["pconst"], ["pconst"])
        pool_const(trirev[:], 1.0, ([[-1, 128]], ALU.is_gt, 1))
        POOL(lambda e: e.memset(trirev[64:128, 0:64], 0.0), ["pconst"], ["pconst"])
        pool_const(blk64[:], 0.0)
        POOL(lambda e: e.memset(blk64[0:64, 0:64], 1.0), ["pconst"], ["pconst"])
        POOL(lambda e: e.memset(blk64[64:128, 64:128], 1.0), ["pconst"], ["pconst"])
        POOL(lambda e: e.memset(S32[:], 0.0), [], [f"S{h}" for h in range(4)])
        POOL(lambda e: e.memset(Sbf[:], 0.0), [], [f"S{h}" for h in range(4)])
        POOL(lambda e: e.memset(cols[:, 3:4], EPS), [], ["pconst"])

        ACT(lambda e: e.activation(out=ss[:, 15:16], in_=ss[:, 15:16], func=AF.Exp), [], ["ssd"])
        def late_consts():
            DVE(lambda e: e.tensor_tensor(out=lbtmp[:, 0, :], in0=lbtmp[:, 0, :], in1=lbtmp[:, 1, :], op=ALU.subtract),
                ["const"], ["lbt"])
            ACT(lambda e: e.activation(out=lb_bc[:], in_=lbtmp[:, 0, :], func=AF.Exp, scale=-1.0), ["lbt"], ["lb1"])
            DVE(lambda e: e.tensor_scalar(out=lb_bc[:], in0=lb_bc[:], scalar1=1.0, scalar2=None, op0=ALU.add),
                ["lb1"], ["lb2"])
            DVE(lambda e: e.reciprocal(out=lb_bc[:], in_=lb_bc[:]), ["lb2"], ["lb3"])
            DVE(lambda e: e.tensor_scalar(out=oml_bc[:], in0=lb_bc[:], scalar1=-1.0, scalar2=1.0, op0=ALU.mult,
                                          op1=ALU.add), ["lb3"], ["const2"])
            DVE(lambda e: e.tensor_tensor(out=cols[:, 8:12], in0=cols[:, 12:16], in1=cols[:, 8:12], op=ALU.subtract),
                ["const"], ["colt"])
            ACT(lambda e: e.activation(out=cols[:, 8:12], in_=cols[:, 8:12], func=AF.Exp, scale=-1.0),
                ["colt"], ["colt2"])
            DVE(lambda e: e.tensor_scalar(out=cols[:, 8:12], in0=cols[:, 8:12], scalar1=1.0, scalar2=None, op0=ALU.add),
                ["colt2"], ["colt3"])
            DVE(lambda e: e.reciprocal(out=cols[:, 4:8], in_=cols[:, 8:12]), ["colt3"], ["const2"])
            DVE(lambda e: e.tensor_scalar(out=cols[:, 0:1], in0=cols[:, 0:1], scalar1=0.125, scalar2=None, op0=ALU.mult),
                ["const"], ["const2"])

        CONST = ["const", "const2", "pconst"]

        def slab_list():
            L = []
            W2 = lambda wd_, c0, n: wd_[:, c0:c0 + n].rearrange("(kc p) n -> p kc n", p=128)
            for j in (3, 4, 5, 6, 0, 1, 2):
                L.append(("in", j, W2(win_d, j * 512, 512), (8, 512)))
            for j in range(2):
                L.append(("br", j, W2(wbr_d, j * 512, 512), (8, 512)))
                L.append(("gs", j, W2(win_d, 3584 + j * 512, 512), (8, 512)))
                L.append(("gh", j, W2(win_d, 4608 + j * 512, 512), (8, 512)))
            for j in range(2):
                L.append(("out", j, W2(wout_d, j * 512, 512), (8, 512)))
            for j in range(6):
                n = 512 if j < 5 else 256
                L.append(("fg", j, W2(wg_d, j * 512, n), (8, n)))
                L.append(("fu", j, W2(wu_d, j * 512, n), (8, n)))
            for p in range(2):
                for (f0, nf) in ((0, 8), (8, 8), (16, 6)):
                    src = wd_d[f0 * 128:(f0 + nf) * 128, p * 512:(p + 1) * 512].rearrange("(kc p) n -> p kc n", p=128)
                    L.append(("fd", (p, f0, nf), src, (nf, 512)))
            return L

        per_tile = slab_list()
        NSL = len(per_tile)
        state = {"issued": 0}
        TOTAL = NSL * NT
        scratch = []
        for k, (_, _, src, (a, b)) in enumerate(per_tile):
            sc = nc.dram_tensor(f"wsc{k}", [128, a * b], BF16).ap()
            scratch.append(sc)
            sems[f"s_cast{k}"] = es.enter_context(nc.semaphore(f"s_cast{k}"))
            sch.op("pool", (lambda v, s_: (lambda e: e.dma_start(out=v, in_=s_)))(
                sc.rearrange("p (a b) -> p a b", a=a), src), r=([f"wsc{k - 4}"] if k >= 4 else []),
                w=[f"wsc{k}"], dma_sem=f"s_cast{k}")

        def issue_until(idx):
            while state["issued"] <= min(idx, TOTAL - 1):
                i = state["issued"]
                k = i % NSL
                _, _, _, (a, b) = per_tile[k]
                slot = i % NSLOT
                view = ring[slot][:, 0:a * b]
                sch.op("sp", (lambda v, s_: (lambda e: e.dma_start(out=v, in_=s_)))(view, scratch[k][:, :]),
                       r=[f"wsc{k}"], w=[f"ring{slot}"], dma_sem=f"s_w{slot}")
                state["issued"] += 1

        cur = {"i": 0}

        def next_group(kinds):
            i0 = cur["i"]
            assert len(kinds) <= NSLOT
            issue_until(i0 + NSLOT - 1)
            outs = []
            for expect in kinds:
                i = cur["i"]
                cur["i"] += 1
                kind, j, _, (a, b) = per_tile[i % NSL]
                assert kind == expect, (kind, expect)
                slot = i % NSLOT
                view = ring[slot][:, 0:a * b].rearrange("p (a b) -> p a b", a=a)
                outs.append((view, f"ring{slot}"))
            return outs

        def next_slab(expect):
            return next_group([expect])[0]

        bank_rr = {"i": 0}

        def tmp_bank(pool=(0, 1, 2, 3, 4, 5, 6, 7)):
            b = pool[bank_rr["i"] % len(pool)]
            bank_rr["i"] += 1
            return b

        def bk(b):
            return f"ps{b}"

        junkv = aview(R3 + 39 * KB, [128, D], F32)

        def rmsnorm_T(ti, gbc, gkey):
            for tc in range(4):
                ACT(lambda e, tc=tc: e.activation(out=junkv, in_=x_sb[:, tc, :], func=AF.Square,
                                                  accum_out=ss[:, tc:tc + 1]), [f"x{tc}"], ["tA", "tB", f"ss{tc}"])
            ACT(lambda e: e.activation(out=ss[:, 4:8], in_=ss[:, 0:4], func=AF.Ln, scale=1.0 / D, bias=EPS),
                [f"ss{c}" for c in range(4)], ["ssl"])
            ACT(lambda e: e.activation(out=ss[:, 8:12], in_=ss[:, 4:8], func=AF.Exp, scale=-0.5), ["ssl"], ["ssr"])
            for tc in range(4):
                xb = xn_tm[:, tc % 2, :]
                xk = f"xn_tm{tc % 2}"
                DVE(lambda e, tc=tc, xb=xb: e.scalar_tensor_tensor(out=xb, in0=x_sb[:, tc, :],
                                                                   scalar=ss[:, 8 + tc:9 + tc], in1=gbc[:],
                                                                   op0=ALU.mult, op1=ALU.mult),
                    [f"x{tc}", "ssr", gkey], [xk])
                b = tmp_bank()
                psb = ps[b][:].bitcast(BF16)

                def tr(e, psb=psb, xb=xb):
                    ins = None
                    for dc in range(8):
                        ins = e.transpose(out=psb[:, dc * 128:(dc + 1) * 128], in_=xb[:, dc * 128:(dc + 1) * 128],
                                          identity=ident[:])
                    return ins
                PE(tr, [xk, "pconst"], [bk(b)])
                ACT(lambda e, tc=tc, psb=psb: e.activation(out=xnT[:, :, tc * 128:(tc + 1) * 128],
                                                           in_=psb.rearrange("p (a b) -> p a b", a=8),
                                                           func=AF.Copy), [bk(b)], ["xnT"])

        def proj_fm(slab, skey, cc, b, rhsT=None, rkey="xnT", nk=8, koff=0):
            rhsT = xnT if rhsT is None else rhsT
            pairs = [(slab[:, koff + kc, cc * 128:(cc + 1) * 128], rhsT[:, kc, :]) for kc in range(nk)]
            PE(mm_group(ps[b][:], pairs), [skey, rkey], [bk(b)])

        def proj_tm(slab, skey, tc, b, ncols=512):
            pairs = [(xnT[:, kc, tc * 128:(tc + 1) * 128], slab[:, kc, 0:ncols]) for kc in range(8)]
            PE(mm_group(ps[b][:, 0:ncols], pairs), [skey, "xnT"], [bk(b)])

        def sigmoid_from_exp(buf, key):
            ACT(lambda e: e.activation(out=buf, in_=buf, func=AF.Ln, bias=1.0), [key], [key])
            ACT(lambda e: e.activation(out=buf, in_=buf, func=AF.Exp, scale=-1.0), [key], [key])

        for ti in range(NT):
            for tc in range(4):
                if ti == 0:
                    break
                sch.op("sp", lambda e, ti=ti, tc=tc: e.dma_start(
                    out=x_sb[:, tc, :], in_=x_d[ti * TQ + tc * 128:ti * TQ + (tc + 1) * 128, :]),
                    r=[], w=[f"x{tc}"], dma_sem=f"s_x{tc}")
            rmsnorm_T(ti, g1_bc, "g1")
            if ti == 0:
                late_consts()

            slab, skey = next_slab("in")
            for h in range(4):
                a = tmp_bank()
                proj_fm(slab, skey, h, a)
                ACT(lambda e, a=a: e.activation(out=tA[:], in_=ps[a][:], func=AF.Exp, scale=-1.0), [bk(a)], ["tA"])
                sigmoid_from_exp(tA[:], "tA")
                DVE(lambda e, a=a, h=h: e.tensor_tensor(out=qhT[:, h, :], in0=ps[a][:], in1=tA[:], op=ALU.mult),
                    [bk(a), "tA"], [f"qhT{h}"])
            slab, skey = next_slab("in")
            for h in range(4):
                a = tmp_bank()
                proj_fm(slab, skey, h, a)
                ACT(lambda e, a=a: e.activation(out=tA[:], in_=ps[a][:], func=AF.Exp, scale=1.0), [bk(a)], ["tA"])
                sigmoid_from_exp(tA[:], "tA")
                DVE(lambda e, h=h: e.tensor_scalar(out=keyT[:, h, :], in0=tA[:], scalar1=cols[:, 4 + h:5 + h],
                                                   scalar2=None, op0=ALU.mult), ["tA"] + CONST, [f"keyT{h}"])
            for tc in range(4):
                a = tmp_bank()
                proj_tm(slab, skey, tc, a)
                ACT(lambda e, a=a: e.activation(out=tB[:], in_=ps[a][:], func=AF.Exp, scale=1.0), [bk(a)], ["tB"])
                sigmoid_from_exp(tB[:], "tB")
                DVE(lambda e: e.tensor_tensor(out=tB[:], in0=tB[:], in1=oml_bc[:], op=ALU.mult), ["tB"] + CONST,
                    ["tB"])
                DVE(lambda e, tc=tc: e.tensor_copy(out=key_tm[:, tc, :], in_=tB[:]), ["tB"], [f"key_tm{tc}"])
                ACT(lambda e, tc=tc: e.activation(out=logf[:, tc, :], in_=tB[:], func=AF.Ln, scale=-1.0, bias=1.0),
                    ["tB"], [f"logf{tc}"])
            slab, skey = next_slab("in")
            for tc in range(4):
                a = tmp_bank()
                proj_tm(slab, skey, tc, a)
                ACT(lambda e, a=a, tc=tc: e.activation(out=v_tm[:, tc, :], in_=ps[a][:], func=AF.Copy),
                    [bk(a)], [f"v_tm{tc}"])
            slab, skey = next_slab("in")
            for h in range(4):
                a = tmp_bank()
                proj_fm(slab, skey, h, a)
                ACT(lambda e, a=a: e.activation(out=tA[:], in_=ps[a][:], func=AF.Exp, scale=-1.0), [bk(a)], ["tA"])
                sigmoid_from_exp(tA[:], "tA")
                DVE(lambda e, a=a, h=h: e.tensor_tensor(out=ghT[:, h, :], in0=ps[a][:], in1=tA[:], op=ALU.mult),
                    [bk(a), "tA"], [f"ghT{h}"])

            qk_items = []
            for which, (slab, skey) in enumerate(next_group(["in", "in"])):
                for hp in range(4):
                    qk_items.append((which, hp, slab, skey))
            sqb = [(sq_sb, "sq_sb"), (kd, "kd")]
            rsb = [(tA, "tA"), (tB, "tB")]
            qk_state = {}

            def qk_a(i):
                which, hp, slab, skey = qk_items[i]
                a = tmp_bank()
                qk_state[i] = a
                proj_fm(slab, skey, hp, a)
                sq, sqk = sqb[i % 2]
                ACT(lambda e, a=a, sq=sq: e.activation(out=sq[:], in_=ps[a][:], func=AF.Square), [bk(a)], [sqk])

            def qk_b(i):
                which, hp, slab, skey = qk_items[i]
                a = qk_state[i]
                sq, sqk = sqb[i % 2]
                rs, rsk = rsb[i % 2]
                b2 = tmp_bank()
                PE(mm_group(ps[b2][:], [(blk64[:], sq[:])]), [sqk] + CONST, [bk(b2)])
                ACT(lambda e, b2=b2, rs=rs: e.activation(out=rs[:], in_=ps[b2][:], func=AF.Ln, scale=1.0 / 64,
                                                         bias=EPS), [bk(b2)] + CONST, [rsk])
                ACT(lambda e, rs=rs: e.activation(out=rs[:], in_=rs[:], func=AF.Exp, scale=-0.5), [rsk], [rsk])
                if which == 0:
                    DVE(lambda e, a=a, hp=hp, rs=rs: e.scalar_tensor_tensor(out=QT[:, hp, :], in0=ps[a][:],
                                                                            scalar=cols[:, 0:1], in1=rs[:],
                                                                            op0=ALU.mult, op1=ALU.mult),
                        [bk(a), rsk] + CONST, [f"QT{hp}"])
                else:
                    DVE(lambda e, a=a, hp=hp, ti=ti, rs=rs: e.scalar_tensor_tensor(
                        out=KT[:, hp, ti * TQ:(ti + 1) * TQ], in0=ps[a][:], scalar=cols[:, 1:2], in1=rs[:],
                        op0=ALU.mult, op1=ALU.mult), [bk(a), rsk] + CONST, [f"KT{hp}.{ti}"])

            qk_a(0)
            for i in range(8):
                if i + 1 < 8:
                    qk_a(i + 1)
                qk_b(i)
            slab, skey = next_slab("in")
            for tc in range(4):
                a = tmp_bank()
                proj_tm(slab, skey, tc, a)
                ACT(lambda e, a=a, tc=tc, ti=ti: e.activation(out=V[:, ti * 4 + tc, :], in_=ps[a][:], func=AF.Copy),
                    [bk(a)], [f"V{ti * 4 + tc}"])
            blocks = [(hp, kb) for hp in range(4) for kb in range(ti * 4 + 3, -1, -1)]
            NB = len(blocks)
            Z3 = pp3[0]
            ZK = [bk(0), bk(1)]
            R3v = [pp3[1], pp3[2]]
            RK = [[bk(2), bk(3)], [bk(4), bk(5)]]

            def c0_of(kb):
                j = kb - ti * 4
                return 128 * j if j >= 0 else 0

            def st1(n):
                hp, kb = blocks[n]
                c0 = c0_of(kb)

                def fn(e):
                    ins = None
                    for hh in range(2):
                        rows = slice(hh * 64, hh * 64 + 64)
                        ins = e.matmul(Z3[:, hh, c0:512], lhsT=KT[rows, hp, kb * 128:(kb + 1) * 128],
                                       rhs=QT[rows, hp, c0:512], start=True, stop=True, skip_group_check=True)
                    return ins
                PE(fn, [f"KT{hp}.{kb // 4}", f"QT{hp}"], ZK)
                p = n % 2
                ACT(lambda e: e.activation(out=e_sb[:, 2 * p:2 * p + 2, c0:512], in_=Z3[:, :, c0:512], func=AF.Exp),
                    ZK, [f"e{p}"])

            def st1b(n):
                hp, kb = blocks[n]
                c0 = c0_of(kb)
                p = n % 2
                ACT(lambda e: e.activation(out=sp_sb[:, 2 * p:2 * p + 2, c0:512], in_=e_sb[:, 2 * p:2 * p + 2, c0:512],
                                           func=AF.Ln, bias=1.0), [f"e{p}"], [f"sp{p}"])
                if kb >= ti * 4:
                    DVE(lambda e: e.tensor_tensor(out=sp_sb[:, 2 * p:2 * p + 2, c0:c0 + 128],
                                                  in0=sp_sb[:, 2 * p:2 * p + 2, c0:c0 + 128], in1=mdiag2[:],
                                                  op=ALU.mult), [f"sp{p}"] + CONST, [f"sp{p}"])

            def st2(n):
                hp, kb = blocks[n]
                c0 = c0_of(kb)
                p = n % 2
                Rv = R3v[p]
                first = (kb == ti * 4 + 3)
                if first:
                    DVE(lambda e: e.memset(spcum[:], 0.0), [], ["spcum0", "spcum1"])
                k_idx = (ti * 4 + 3) - kb
                cp = k_idx % 2

                def fn(e):
                    ins = None
                    for hh in range(2):
                        rows = slice(hh * 64, hh * 64 + 64)
                        ins = e.matmul(Rv[:, hh, c0:512], lhsT=KT[rows, hp, kb * 128:(kb + 1) * 128],
                                       rhs=QT[rows, hp, c0:512], start=True, stop=False, skip_group_check=True)
                    return ins
                PE(fn, [f"KT{hp}.{kb // 4}", f"QT{hp}"], RK[p])
                rk = [f"sp{p}"] + CONST + ([] if first else [f"spcum{cp}"])

                def fn2(e):
                    ins = None
                    for hh in range(2):
                        ins = e.matmul(Rv[:, hh, c0:512], lhsT=lincl[:], rhs=sp_sb[:, 2 * p + hh, c0:512], start=False,
                                       stop=first, skip_group_check=True)
                        if not first:
                            ins = e.matmul(Rv[:, hh, c0:512], lhsT=nones_bf[:], rhs=spcum[:, 2 * cp + hh, c0:512],
                                           start=False, stop=True, skip_group_check=True)
                    return ins
                PE(fn2, rk, RK[p])
                if kb > 0:
                    DVE(lambda e: e.tensor_tensor(out=spcum[:, 2 * (1 - cp):2 * (1 - cp) + 2, c0:512],
                                                  in0=spcum[:, 2 * cp:2 * cp + 2, c0:512],
                                                  in1=sp_sb[:, 2 * p:2 * p + 2, c0:512], op=ALU.add),
                        [f"spcum{cp}", f"sp{p}"], [f"spcum{1 - cp}"])

            def st2b(n):
                hp, kb = blocks[n]
                c0 = c0_of(kb)
                p = n % 2
                Rv = R3v[p]
                ACT(lambda e: e.activation(out=w_sb[:, 2 * p:2 * p + 2, c0:512], in_=Rv[:, :, c0:512], func=AF.Exp),
                    RK[p], [f"w{p}"])
                if kb >= ti * 4:
                    DVE(lambda e: e.tensor_tensor(out=w_sb[:, 2 * p:2 * p + 2, c0:c0 + 128],
                                                  in0=w_sb[:, 2 * p:2 * p + 2, c0:c0 + 128], in1=mdiag2[:],
                                                  op=ALU.mult), [f"w{p}"] + CONST, [f"w{p}"])

            def st3(n):
                hp, kb = blocks[n]
                c0 = c0_of(kb)
                p = n % 2
                ob = 6 + (hp % 2)
                first = (kb == ti * 4 + 3)

                def fn(e):
                    ins = None
                    for hh in range(2):
                        h = 2 * hp + hh
                        ins = e.matmul(ps[ob][hh * 64:hh * 64 + 64, c0:512], lhsT=V[:, kb, h * 64:(h + 1) * 64],
                                       rhs=w_sb[:, 2 * p + hh, c0:512], start=first, stop=(kb == 0),
                                       skip_group_check=True)
                    return ins
                PE(fn, [f"V{kb}", f"w{p}"], [bk(ob)])
                if kb == 0:
                    ACT(lambda e: e.activation(out=o_sbT[:, hp, :], in_=ps[ob][:], func=AF.Copy),
                        [bk(ob)], [f"o_sbT{hp}"])

            for i in range(NB + 3):
                if i < NB:
                    st1(i)
                if 0 <= i - 2 < NB:
                    st2b(i - 2)
                if i < NB:
                    st1b(i)
                if 0 <= i - 1 < NB:
                    st2(i - 1)
                if 0 <= i - 3 < NB:
                    st3(i - 3)

            for tc in range(4):
                tsl = slice(tc * 128, (tc + 1) * 128)
                bB, bD, bA, bO = 0, 1, 2, 3
                def fnb(e, tc=tc):
                    ins = None
                    for h in range(4):
                        ins = e.matmul(ps[bB][:, h * 128:(h + 1) * 128], lhsT=logf[:, tc, h * 128:(h + 1) * 128],
                                       rhs=triblk[:], start=True, stop=True, skip_group_check=True)
                    return ins
                PE(fnb, [f"logf{tc}"] + CONST, [bk(bB)])
                PE(mm_group(ps[bD][:], [(trirev[:], logf[:, tc, :])]), [f"logf{tc}"] + CONST, [bk(bD)])
                ACT(lambda e: e.activation(out=EbT[:], in_=ps[bB][:].rearrange("p (a b) -> p a b", a=4), func=AF.Exp),
                    [bk(bB)], ["EbT"])
                ACT(lambda e: e.activation(out=EnbT[:], in_=ps[bB][:].rearrange("p (a b) -> p a b", a=4),
                                           func=AF.Exp, scale=-1.0), [bk(bB)], ["EnbT"])
                ACT(lambda e: e.activation(out=Ed[:], in_=ps[bD][:], func=AF.Exp), [bk(bD)], ["Ed"])
                DVE(lambda e, tsl=tsl: e.tensor_tensor(out=qeT[:], in0=qhT[:, :, tsl], in1=EbT[:], op=ALU.mult),
                    ["EbT"] + [f"qhT{h}" for h in range(4)], ["qeT"])
                DVE(lambda e, tsl=tsl: e.tensor_tensor(out=keT[:], in0=keyT[:, :, tsl], in1=EnbT[:], op=ALU.mult),
                    ["EnbT"] + [f"keyT{h}" for h in range(4)], ["keT"])
                DVE(lambda e, tc=tc: e.tensor_tensor(out=kd[:], in0=key_tm[:, tc, :], in1=Ed[:], op=ALU.mult),
                    ["Ed", f"key_tm{tc}"], ["kd"])
                def fna(e):
                    ins = None
                    for h in range(4):
                        ins = e.matmul(ps[bA][:, h * 128:(h + 1) * 128], lhsT=keT[:, h, :], rhs=qeT[:, h, :],
                                       start=True, stop=True, skip_group_check=True)
                    return ins
                PE(fna, ["keT", "qeT"], [bk(bA)])
                for h in range(4):
                    DVE(lambda e, h=h: e.tensor_tensor(out=attn[:, h, :], in0=ps[bA][:, h * 128:(h + 1) * 128],
                                                       in1=triblk[:], op=ALU.mult), [bk(bA)] + CONST, [f"attn{h}"])
                for c in range(2):
                    for h in range(4):
                        hs = slice(h * 128, (h + 1) * 128)
                        csl = slice(c * 64, (c + 1) * 64)
                        osl = slice(h * 128 + c * 64, h * 128 + (c + 1) * 64)
                        PE(mm_group(ps[bO][:, osl], [(v_tm[:, tc, hs], attn[:, h, csl]), (Sbf[:, h, :], qeT[:, h, csl])]),
                           [f"v_tm{tc}", f"attn{h}", f"S{h}", "qeT"], [bk(bO)])
                        sbk = 4 + h
                        PE(mm_group(ps[sbk][:, 0:128], [(kd[csl, hs], v_tm[csl, tc, hs])]), ["kd", f"v_tm{tc}"],
                           [bk(sbk)])
                        last = c * 64 + 63
                        DVE(lambda e, h=h, sbk=sbk, last=last: e.scalar_tensor_tensor(
                            out=S32[:, h, :], in0=S32[:, h, :], scalar=EbT[:, h, last:last + 1], in1=ps[sbk][:, 0:128],
                            op0=ALU.mult, op1=ALU.add), [f"S{h}", "EbT", bk(sbk)], [f"S{h}"])
                        DVE(lambda e, h=h: e.tensor_copy(out=Sbf[:, h, :], in_=S32[:, h, :]), [f"S{h}"], [f"S{h}"])
                ACT(lambda e: e.activation(out=sq_sb[:], in_=ps[bO][:], func=AF.Square), [bk(bO)], ["sq_sb"])
                PE(mm_group(ps[bD][:], [(ones_bf[:], sq_sb[:])]), ["sq_sb"] + CONST, [bk(bD)])
                ACT(lambda e: e.activation(out=tA[:], in_=ps[bD][:], func=AF.Ln, scale=1.0 / 128, bias=EPS),
                    [bk(bD)] + CONST, ["tA"])
                ACT(lambda e: e.activation(out=tA[:], in_=tA[:], func=AF.Exp, scale=-0.5), ["tA"], ["tA"])
                DVE(lambda e: e.scalar_tensor_tensor(out=tB[:], in0=ps[bO][:], scalar=cols[:, 2:3], in1=tA[:],
                                                     op0=ALU.mult, op1=ALU.mult), [bk(bO), "tA"] + CONST, ["tB"])
                DVE(lambda e, tsl=tsl: e.tensor_tensor(out=o_hgT[:, :, tsl],
                                                       in0=tB[:].rearrange("p (a b) -> p a b", a=4),
                                                       in1=ghT[:, :, tsl], op=ALU.mult),
                    ["tB"] + [f"ghT{h}" for h in range(4)], ["o_hgT"])

            if dbg:
                sch.op("sp", lambda e, ti=ti: e.dma_start(out=dbg_osb[ti], in_=o_sbT.rearrange("p a b -> p (a b)")),
                       r=[f"o_sbT{hp}" for hp in range(4)], w=[], dma_sem="s_d1")
                sch.op("sp", lambda e, ti=ti: e.dma_start(out=dbg_ohg[ti], in_=o_hgT.rearrange("p a b -> p (a b)")),
                       r=["o_hgT"], w=[], dma_sem="s_d2")

            for j in range(2):
                (sl_br, k_br), (sl_gs, k_gs), (sl_gh, k_gh) = next_group(["br", "gs", "gh"])
                for cc in range(4):
                    dcc = j * 4 + cc
                    b_ys, b_yh, b_gs, b_gh = 0, 1, 2, 3
                    if dcc % 2 == 1:
                        b_ys, b_yh, b_gs, b_gh = 4, 5, 6, 7
                    proj_fm(sl_br, k_br, cc, b_ys, rhsT=o_sbT, rkey="o_sbT", nk=4, koff=0)
                    proj_fm(sl_br, k_br, cc, b_yh, rhsT=o_hgT, rkey="o_hgT", nk=4, koff=4)
                    proj_fm(sl_gs, k_gs, cc, b_gs)
                    proj_fm(sl_gh, k_gh, cc, b_gh)
                    g0, g1 = gtmp[:, 0, :], gtmp[:, 1, :]
                    ACT(lambda e, b=b_gs, g0=g0: e.activation(out=g0, in_=ps[b][:], func=AF.Exp, scale=-1.0),
                        [bk(b_gs)], ["g0"])
                    ACT(lambda e, b=b_gh, g1=g1: e.activation(out=g1, in_=ps[b][:], func=AF.Exp, scale=-1.0),
                        [bk(b_gh)], ["g1"])
                    ACT(lambda e, g0=g0: e.activation(out=g0, in_=g0, func=AF.Ln, bias=1.0), ["g0"], ["g0"])
                    ACT(lambda e, g1=g1: e.activation(out=g1, in_=g1, func=AF.Ln, bias=1.0), ["g1"], ["g1"])
                    ACT(lambda e, g0=g0: e.activation(out=g0, in_=g0, func=AF.Exp, scale=-1.0), ["g0"], ["g0"])
                    ACT(lambda e, g1=g1: e.activation(out=g1, in_=g1, func=AF.Exp, scale=-1.0), ["g1"], ["g1"])
                    DVE(lambda e, b=b_ys, g0=g0: e.tensor_tensor(out=g0, in0=ps[b][:], in1=g0, op=ALU.mult),
                        [bk(b_ys), "g0"], ["g0"])
                    DVE(lambda e, b=b_yh, g1=g1: e.tensor_tensor(out=g1, in0=ps[b][:], in1=g1, op=ALU.mult),
                        [bk(b_yh), "g1"], ["g1"])
                    DVE(lambda e, dcc=dcc, g0=g0, g1=g1: e.tensor_tensor(out=mixT[:, dcc, :], in0=g0, in1=g1,
                                                                         op=ALU.add), ["g0", "g1"], [f"mixT{dcc}"])
            (sl_o0, k_o0), (sl_o1, k_o1) = next_group(["out", "out"])
            for tc in range(4):
                for j, (sl_o, k_o) in enumerate(((sl_o0, k_o0), (sl_o1, k_o1))):
                    b = tmp_bank((0, 1, 2, 3, 4, 5, 6, 7))
                    pairs = [(mixT[:, dcc, tc * 128:(tc + 1) * 128], sl_o[:, dcc, :]) for dcc in range(8)]
                    PE(mm_group(ps[b][:], pairs), [k_o] + [f"mixT{d_}" for d_ in range(8)], [bk(b)])
                    DVE(lambda e, b=b, tc=tc, j=j: e.tensor_tensor(out=x_sb[:, tc, j * 512:(j + 1) * 512],
                                                                   in0=ps[b][:], in1=x_sb[:, tc, j * 512:(j + 1) * 512],
                                                                   op=ALU.add), [bk(b), f"x{tc}"], [f"x{tc}"])
            if dbg:
                sch.op("sp", lambda e, ti=ti: e.dma_start(
                    out=dbg_h[ti * TQ:(ti + 1) * TQ, :].rearrange("(tc p) d -> p tc d", p=128), in_=x_sb[:]),
                    r=[f"x{tc}" for tc in range(4)], w=[], dma_sem="s_d3")

            rmsnorm_T(ti, g2_bc, "g2")
            for j in range(6):
                (sl_g, k_g), (sl_u, k_u) = next_group(["fg", "fu"])
                ncc = 4 if j < 5 else 2
                for cc in range(ncc):
                    fc = j * 4 + cc
                    bg = tmp_bank((0, 1, 2, 3, 4, 5, 6, 7))
                    bu = tmp_bank((0, 1, 2, 3, 4, 5, 6, 7))
                    proj_fm(sl_g, k_g, cc, bg)
                    proj_fm(sl_u, k_u, cc, bu)
                    f0 = ftmp[:, fc % 2, :]
                    fk = f"ft{fc % 2}"
                    ACT(lambda e, bg=bg, f0=f0: e.activation(out=f0, in_=ps[bg][:], func=AF.Exp, scale=-1.0),
                        [bk(bg)], [fk])
                    sigmoid_from_exp(f0, fk)
                    DVE(lambda e, bg=bg, f0=f0: e.tensor_tensor(out=f0, in0=ps[bg][:], in1=f0, op=ALU.mult),
                        [bk(bg), fk], [fk])
                    DVE(lambda e, bu=bu, f0=f0, fc=fc: e.tensor_tensor(out=actT[:, fc, :], in0=ps[bu][:], in1=f0,
                                                                       op=ALU.mult), [bk(bu), fk], [f"actT{fc}"])
            for p in range(2):
                banks = (0, 1, 2, 3) if p == 0 else (4, 5, 6, 7)
                for gi, (f0_, nf) in enumerate(((0, 8), (8, 8), (16, 6))):
                    sl_d, k_d = next_slab("fd")
                    for tc in range(4):
                        b = banks[tc]
                        pairs = [(actT[:, f0_ + k, tc * 128:(tc + 1) * 128], sl_d[:, k, :]) for k in range(nf)]

                        def fn(e, pairs=pairs, b=b, gi=gi):
                            ins = None
                            for i, (l, rr) in enumerate(pairs):
                                ins = e.matmul(ps[b][:], lhsT=l, rhs=rr, start=(gi == 0 and i == 0),
                                               stop=(gi == 2 and i == len(pairs) - 1), skip_group_check=True)
                            return ins
                        PE(fn, [k_d] + [f"actT{f0_ + k}" for k in range(nf)], [bk(b)])
                for tc in range(4):
                    b = banks[tc]
                    DVE(lambda e, b=b, tc=tc, p=p: e.tensor_tensor(out=x_sb[:, tc, p * 512:(p + 1) * 512],
                                                                   in0=ps[b][:], in1=x_sb[:, tc, p * 512:(p + 1) * 512],
                                                                   op=ALU.add), [bk(b), f"x{tc}"], [f"x{tc}"])
            for tc in range(4):
                sch.op("sp", lambda e, ti=ti, tc=tc: e.dma_start(
                    out=out_d[ti * TQ + tc * 128:ti * TQ + (tc + 1) * 128, :], in_=x_sb[:, tc, :]),
                    r=[f"x{tc}"], w=[], dma_sem=f"s_o{tc}")

        final_o = [sch.count.get(f"s_o{i}", 0) for i in range(4)]

        with nc.Block() as block:
            def emit(e, name):
                for waits, fn, sem, inc in sch.ops[name]:
                    for s, v in waits:
                        e.wait_ge(sems[s], v)
                    ins = fn(e)
                    ins.then_inc(sems[sem], inc)

            @block.sync
            def _(e):
                emit(e, "sp")
                for i in range(4):
                    e.wait_ge(sems[f"s_o{i}"], final_o[i])

            @block.gpsimd
            def _(e):
                emit(e, "pool")

            @block.scalar
            def _(e):
                emit(e, "act")

            @block.vector
            def _(e):
                emit(e, "dve")

            @block.tensor
            def _(e):
                emit(e, "pe")
    return nc


_CACHE = {}


def run(inputs, S, n_cores, dbg=False):
    key = (S, dbg)
    if key not in _CACHE:
        _CACHE[key] = build(S, dbg)
    nc = _CACHE[key]
    f = lambda a: np.ascontiguousarray(np.asarray(a, dtype=np.float32))
    shared = {
        "norm1_g": f(inputs["norm1_g"]).reshape(1, D),
        "w_in": f(inputs["w_in"]).reshape(D, INW),
        "q_norm_g": f(inputs["q_norm_g"]).reshape(1, 64),
        "k_norm_g": f(inputs["k_norm_g"]).reshape(1, 64),
        "lb_table": f(inputs["lb_table"]).reshape(2, 512),
        "hg_norm_g": f(inputs["hg_norm_g"]).reshape(1, 128),
        "w_branch": f(inputs["w_branch"]).reshape(D, D),
        "w_out": f(inputs["w_out"]).reshape(D, D),
        "norm2_g": f(inputs["norm2_g"]).reshape(1, D),
        "w_ffn_gate": f(inputs["w_ffn_gate"]).reshape(D, FF),
        "w_ffn_up": f(inputs["w_ffn_up"]).reshape(D, FF),
        "w_ffn_down": f(inputs["w_ffn_down"]).reshape(FF, D),
    }
    x = f(inputs["x"])
    in_maps = [dict(shared, x=x[i]) for i in range(n_cores)]
    res = run_bass_kernel_spmd(nc, in_maps, core_ids=list(range(n_cores)))
    return res.results


def kernel(x, norm1_g, w_in, q_norm_g, k_norm_g, lb_table, hg_norm_g, w_branch, w_out, norm2_g,
           w_ffn_gate, w_ffn_up, w_ffn_down):
    inputs = dict(x=x, norm1_g=norm1_g, w_in=w_in, q_norm_g=q_norm_g, k_norm_g=k_norm_g, lb_table=lb_table,
                  hg_norm_g=hg_norm_g, w_branch=w_branch, w_out=w_out, norm2_g=norm2_g, w_ffn_gate=w_ffn_gate,
                  w_ffn_up=w_ffn_up, w_ffn_down=w_ffn_down)
    B, S, _ = np.asarray(x).shape
    res = run(inputs, S, B)
    return np.stack([r["out"] for r in res], axis=0).astype(np.float32)
```
